# Optimizing a Trainium2 kernel written in Bass

```python
import math
import jax, jax.numpy as jnp
from jax import lax
import numpy as np

D_MODEL = 1024
BATCH = 8
SEQ = 2048
DEPTH = 1

HEAD_DIM = 64
ATTN_HEADS = 8
DILATION_GROUPS = ((128, 1), (512, 4), (2048, 16))
N_GROUPS = len(DILATION_GROUPS)
ATTN_WIDTH = ATTN_HEADS * HEAD_DIM
QKV_WIDTH = N_GROUPS * ATTN_WIDTH
BLOCK = 128
ROPE_THETA = 10000.0
NEG_INF = -1e30
SSM_GROUP = 16
SSM_GROUPS = 32
SSM_WIDTH = SSM_GROUP * SSM_GROUPS
SSM_STATE = 64
DT_MIN = 1e-3
DT_MAX = 1e-1
N_BRANCH = 2
IN_WIDTH = 3 * QKV_WIDTH + SSM_WIDTH + N_BRANCH * D_MODEL
D_FF = -(-(8 * D_MODEL) // (3 * 256)) * 256
DN_ALPHA = (2.0 * DEPTH) ** 0.25
DN_BETA = (8.0 * DEPTH) ** -0.25
LN_EPS = 1e-5

kernel_name = "dilated_attn_s5_gated_hybrid_deepnorm"


def layer_norm(x, g, b):
    xf = x.astype(jnp.float32)
    mu = jnp.mean(xf, axis=-1, keepdims=True)
    var = jnp.mean(jnp.square(xf - mu), axis=-1, keepdims=True)
    y = (xf - mu) * lax.rsqrt(var + LN_EPS)
    return (y * g.astype(jnp.float32) + b.astype(jnp.float32)).astype(x.dtype)


def apply_rope(t, pos):
    half = HEAD_DIM // 2
    inv_freq = ROPE_THETA ** (-jnp.arange(half, dtype=jnp.float32) / half)
    ang = pos[:, None] * inv_freq[None, :]
    cos = jnp.cos(ang)[None, :, None, None, :]
    sin = jnp.sin(ang)[None, :, None, None, :]
    t1 = t[..., :half].astype(jnp.float32)
    t2 = t[..., half:].astype(jnp.float32)
    return jnp.concatenate([t1 * cos - t2 * sin, t1 * sin + t2 * cos], axis=-1)


def dilated_group_attention(q, k, v, window, dilation):
    b, s, h, dh = q.shape
    n = s // dilation
    nb = -(-n // BLOCK)
    n_pad = nb * BLOCK
    back = window // dilation

    def to_phase_blocks(t):
        t = t.reshape(b, n, dilation, h, dh)
        t = jnp.pad(t, ((0, 0), (0, n_pad - n), (0, 0), (0, 0), (0, 0)))
        return t.reshape(b, nb, BLOCK, dilation, h, dh)

    def with_prev_block(t):
        prev = jnp.pad(t, ((0, 0), (1, 0), (0, 0), (0, 0), (0, 0), (0, 0)))[:, :-1]
        return jnp.concatenate([prev, t], axis=2)

    qb = to_phase_blocks(q)
    kw = with_prev_block(to_phase_blocks(k))
    vw = with_prev_block(to_phase_blocks(v))

    scores = jnp.einsum("bnqrhd,bnkrhd->bnrhqk", qb, kw,
                        preferred_element_type=jnp.float32) / math.sqrt(dh)
    a_idx = jnp.arange(BLOCK)[None, :, None]
    c_idx = jnp.arange(2 * BLOCK)[None, None, :]
    blk = jnp.arange(nb)[:, None, None]
    dist = BLOCK + a_idx - c_idx
    key_sub = (blk - 1) * BLOCK + c_idx
    valid = (dist >= 0) & (dist <= back) & (key_sub >= 0)
    scores = jnp.where(valid[None, :, None, None], scores, NEG_INF)
    lse = jax.nn.logsumexp(scores, axis=-1)
    probs = jnp.exp(scores - lse[..., None])
    out = jnp.einsum("bnrhqk,bnkrhd->bnqrhd", probs, vw.astype(jnp.float32))
    out = out.reshape(b, n_pad, dilation, h, dh)[:, :n].reshape(b, s, h, dh)
    lse = jnp.transpose(lse, (0, 1, 4, 2, 3)).reshape(b, n_pad, dilation, h)[:, :n]
    return out, lse.reshape(b, s, h)


def s5_ssm(u, a_re, a_im, log_dt, b_re, b_im, c_re, c_im, d_skip):
    f32 = jnp.float32
    bsz, s, _ = u.shape
    ug = u.astype(f32).reshape(bsz, s, SSM_GROUPS, SSM_GROUP)
    lam = lax.complex(a_re.astype(f32), a_im.astype(f32))
    dt = jnp.exp(log_dt.astype(f32))[:, None]
    a_bar = jnp.exp(lam * dt)
    b_c = lax.complex(b_re.astype(f32), b_im.astype(f32))
    b_bar = ((a_bar - 1.0) / lam)[..., None] * b_c
    bu = jnp.einsum("gph,bsgh->bsgp", b_bar, ug.astype(jnp.complex64))
    a_seq = jnp.broadcast_to(a_bar, bu.shape)

    def combine(left, right):
        a_l, x_l = left
        a_r, x_r = right
        return a_r * a_l, a_r * x_l + x_r

    _, states = lax.associative_scan(combine, (a_seq, bu), axis=1)
    c_c = lax.complex(c_re.astype(f32), c_im.astype(f32))
    y = jnp.einsum("ghp,bsgp->bsgh", c_c, states).real
    y = y + d_skip.astype(f32).reshape(SSM_GROUPS, SSM_GROUP) * ug
    return y.reshape(bsz, s, SSM_WIDTH)


def setup_inputs(seed: int = 0) -> dict:
    key = jax.random.key(seed)
    ks = jax.random.split(key, 24)
    f32 = jnp.float32
    L = DEPTH

    def nrm(k, shape, scale):
        return jax.random.normal(k, shape, f32) * scale

    x = jax.random.normal(ks[0], (BATCH, SEQ, D_MODEL), f32)
    w_in = nrm(ks[1], (L, D_MODEL, IN_WIDTH), D_MODEL ** -0.5)
    b_gate = nrm(ks[2], (L, N_BRANCH, D_MODEL), 0.02)
    w_attn_br = nrm(ks[3], (L, ATTN_WIDTH, D_MODEL), ATTN_WIDTH ** -0.5)
    w_ssm_br = nrm(ks[4], (L, SSM_WIDTH, D_MODEL), SSM_WIDTH ** -0.5)
    w_out = nrm(ks[5], (L, D_MODEL, D_MODEL), DN_BETA * D_MODEL ** -0.5)
    ssm_a_re = -0.5 + nrm(ks[6], (L, SSM_GROUPS, SSM_STATE), 0.01)
    ssm_a_im = (math.pi * jnp.arange(SSM_STATE, dtype=f32))[None, None, :] + nrm(ks[7], (L, SSM_GROUPS, SSM_STATE), 0.01)
    ssm_log_dt = jax.random.uniform(ks[8], (L, SSM_GROUPS), f32, math.log(DT_MIN), math.log(DT_MAX))
    ssm_b_re = nrm(ks[9], (L, SSM_GROUPS, SSM_STATE, SSM_GROUP), (2 * SSM_GROUP) ** -0.5)
    ssm_b_im = nrm(ks[10], (L, SSM_GROUPS, SSM_STATE, SSM_GROUP), (2 * SSM_GROUP) ** -0.5)
    ssm_c_re = nrm(ks[11], (L, SSM_GROUPS, SSM_GROUP, SSM_STATE), SSM_STATE ** -0.5)
    ssm_c_im = nrm(ks[12], (L, SSM_GROUPS, SSM_GROUP, SSM_STATE), SSM_STATE ** -0.5)
    ssm_d = nrm(ks[13], (L, SSM_WIDTH), 1.0)
    w_glu = nrm(ks[14], (L, SSM_WIDTH, 2 * SSM_WIDTH), SSM_WIDTH ** -0.5)
    ln1_g = 1.0 + nrm(ks[15], (L, D_MODEL), 0.02)
    ln1_b = nrm(ks[16], (L, D_MODEL), 0.02)
    w_ff_gate = nrm(ks[17], (L, D_MODEL, D_FF), D_MODEL ** -0.5)
    w_ff_up = nrm(ks[18], (L, D_MODEL, D_FF), D_MODEL ** -0.5)
    w_ff_down = nrm(ks[19], (L, D_FF, D_MODEL), DN_BETA * D_FF ** -0.5)
    ln2_g = 1.0 + nrm(ks[20], (L, D_MODEL), 0.02)
    ln2_b = nrm(ks[21], (L, D_MODEL), 0.02)
    return {"x": x, "w_in": w_in, "b_gate": b_gate, "w_attn_br": w_attn_br,
            "w_ssm_br": w_ssm_br, "w_out": w_out, "ssm_a_re": ssm_a_re,
            "ssm_a_im": ssm_a_im, "ssm_log_dt": ssm_log_dt, "ssm_b_re": ssm_b_re,
            "ssm_b_im": ssm_b_im, "ssm_c_re": ssm_c_re, "ssm_c_im": ssm_c_im,
            "ssm_d": ssm_d, "w_glu": w_glu, "ln1_g": ln1_g, "ln1_b": ln1_b,
            "w_ff_gate": w_ff_gate, "w_ff_up": w_ff_up, "w_ff_down": w_ff_down,
            "ln2_g": ln2_g, "ln2_b": ln2_b}


def reference(x, w_in, b_gate, w_attn_br, w_ssm_br, w_out, ssm_a_re, ssm_a_im,
              ssm_log_dt, ssm_b_re, ssm_b_im, ssm_c_re, ssm_c_im, ssm_d, w_glu,
              ln1_g, ln1_b, w_ff_gate, w_ff_up, w_ff_down, ln2_g, ln2_b):
    bsz, s, _ = x.shape
    pos = jnp.arange(s, dtype=jnp.float32)
    for layer in range(DEPTH):
        proj = x @ w_in[layer]
        q = proj[..., :QKV_WIDTH].reshape(bsz, s, N_GROUPS, ATTN_HEADS, HEAD_DIM)
        k = proj[..., QKV_WIDTH:2 * QKV_WIDTH].reshape(bsz, s, N_GROUPS, ATTN_HEADS, HEAD_DIM)
        v = proj[..., 2 * QKV_WIDTH:3 * QKV_WIDTH].reshape(bsz, s, N_GROUPS, ATTN_HEADS, HEAD_DIM)
        u = proj[..., 3 * QKV_WIDTH:3 * QKV_WIDTH + SSM_WIDTH]
        gate_logits = proj[..., 3 * QKV_WIDTH + SSM_WIDTH:].reshape(bsz, s, N_BRANCH, D_MODEL)
        q = apply_rope(q, pos)
        k = apply_rope(k, pos)

        outs, lses = [], []
        for g, (window, dilation) in enumerate(DILATION_GROUPS):
            o_g, lse_g = dilated_group_attention(q[:, :, g], k[:, :, g], v[:, :, g], window, dilation)
            outs.append(o_g)
            lses.append(lse_g)
        wts = jax.nn.softmax(jnp.stack(lses, axis=0), axis=0)
        attn = jnp.sum(wts[..., None] * jnp.stack(outs, axis=0), axis=0)
        y_attn = attn.reshape(bsz, s, ATTN_WIDTH).astype(x.dtype) @ w_attn_br[layer]

        y_s = s5_ssm(u, ssm_a_re[layer], ssm_a_im[layer], ssm_log_dt[layer], ssm_b_re[layer],
                     ssm_b_im[layer], ssm_c_re[layer], ssm_c_im[layer], ssm_d[layer])
        glu = jax.nn.gelu(y_s).astype(x.dtype) @ w_glu[layer]
        y_s = glu[..., :SSM_WIDTH] * jax.nn.sigmoid(glu[..., SSM_WIDTH:])
        y_ssm = y_s @ w_ssm_br[layer]

        gates = jax.nn.sigmoid((gate_logits + b_gate[layer]).astype(jnp.float32))
        mixed = gates[..., 0, :] * y_attn.astype(jnp.float32) + gates[..., 1, :] * y_ssm.astype(jnp.float32)
        mix_out = mixed.astype(x.dtype) @ w_out[layer]
        h = layer_norm(DN_ALPHA * x + mix_out.astype(x.dtype), ln1_g[layer], ln1_b[layer])

        ff = (jax.nn.silu(h @ w_ff_gate[layer]) * (h @ w_ff_up[layer])) @ w_ff_down[layer]
        x = layer_norm(DN_ALPHA * h + ff.astype(h.dtype), ln2_g[layer], ln2_b[layer])
    return x
```

```python
import math
from contextlib import ExitStack

import numpy as np
import ml_dtypes

import concourse.bass as bass
import concourse.mybir as mybir
from concourse.bass_utils import run_bass_kernel_spmd

F32 = mybir.dt.float32
BF16 = mybir.dt.bfloat16
I32 = mybir.dt.int32
AF = mybir.ActivationFunctionType
ALU = mybir.AluOpType

ENGS = ("pe", "act", "dve", "pool", "sp")

S_LEN = 2048
D = 1024
NKT = 8
DFF = 2816
ALPHA = 2.0 ** 0.25
LN_EPS = 1e-5
DIL = (1, 4, 16)


class Buf:
    __slots__ = ("w", "r", "name", "dsem", "excl")

    def __init__(self, name="", excl=False):
        self.w = {}
        self.r = {}
        self.name = name
        self.dsem = None
        self.excl = excl


class Sched:
    def __init__(self, nc):
        self.nc = nc
        self.streams = {e: [] for e in ENGS}
        self.cnt = {}
        self.sem_objs = {}
        self.sem_names = []
        self.waited = {e: {} for e in ENGS}
        self.nbuf = 0
        for e in ENGS:
            self.new_sem("c_" + e)

    def new_sem(self, name):
        assert name not in self.cnt
        self.cnt[name] = 0
        self.sem_names.append(name)
        return name

    def buf(self, name="", excl=False):
        self.nbuf += 1
        return Buf("%s_%d" % (name, self.nbuf), excl)

    def _deps(self, reads, writes, extra):
        deps = {}

        def add(d):
            for s, v in d.items():
                if deps.get(s, 0) < v:
                    deps[s] = v
        for b in reads:
            add(b.w)
            if b.excl:
                add(b.r)
        for b in writes:
            add(b.w)
            add(b.r)
        for t in extra:
            if t is not None:
                add({t[0]: t[1]})
        return deps

    def _waits(self, eng, deps):
        out = []
        for s, v in deps.items():
            if self.waited[eng].get(s, 0) >= v:
                continue
            self.waited[eng][s] = v
            out.append((s, v))
        return out

    def _mark(self, tok, reads, writes):
        s, v = tok
        for b in reads:
            if b.r.get(s, 0) < v:
                b.r[s] = v
        for b in writes:
            if b.w.get(s, 0) < v:
                b.w[s] = v

    def op(self, eng, fn, reads=(), writes=(), extra=()):
        w = self._waits(eng, self._deps(reads, writes, extra))
        s = "c_" + eng
        self.cnt[s] += 1
        tok = (s, self.cnt[s])
        self.streams[eng].append((w, [fn], tok, 1))
        self._mark(tok, reads, writes)
        return tok

    def group(self, eng, fns, reads=(), writes=(), extra=()):
        w = self._waits(eng, self._deps(reads, writes, extra))
        s = "c_" + eng
        self.cnt[s] += 1
        tok = (s, self.cnt[s])
        self.streams[eng].append((w, list(fns), tok, 1))
        self._mark(tok, reads, writes)
        return tok

    def dma(self, eng, fn, sembuf, reads=(), writes=(), extra=()):
        if sembuf.dsem is None:
            sembuf.dsem = self.new_sem("d_" + sembuf.name)
        deps = self._deps(reads, [], extra)
        for b in writes:
            for d_ in (b.r, b.w):
                for s_, v_ in d_.items():
                    if d_ is b.w and s_ == sembuf.dsem:
                        continue
                    if deps.get(s_, 0) < v_:
                        deps[s_] = v_
        w = self._waits(eng, deps)
        self.cnt[sembuf.dsem] += 16
        tok = (sembuf.dsem, self.cnt[sembuf.dsem])
        self.streams[eng].append((w, [fn], tok, 16))
        self._mark(tok, reads, writes)
        return tok

    def wait_only(self, eng, toks):
        deps = {}
        for t in toks:
            if t is not None and deps.get(t[0], 0) < t[1]:
                deps[t[0]] = t[1]
        w = self._waits(eng, deps)
        if w:
            self.streams[eng].append((w, [], None, 0))

    def barrier(self):
        toks = [(s, v) for s, v in self.cnt.items() if v > 0]
        for e in ENGS:
            self.wait_only(e, toks)

    def build(self):
        nc = self.nc
        with ExitStack() as es:
            for n in self.sem_names:
                self.sem_objs[n] = es.enter_context(nc.semaphore(n))
            block = es.enter_context(nc.Block())
            emap = {"pe": block.tensor, "act": block.scalar, "dve": block.vector,
                    "pool": block.gpsimd, "sp": block.sync}
            for e in ENGS:
                stream = self.streams[e]

                def body(eng, stream=stream):
                    for (w, fns, tok, inc) in stream:
                        for (s, v) in w:
                            eng.wait_ge(self.sem_objs[s], v)
                        ins = None
                        for fn in fns:
                            ins = fn(eng)
                        if tok is not None and ins is not None:
                            ins.then_inc(self.sem_objs[tok[0]], inc)
                emap[e](body)


TWO_PI = 2.0 * math.pi


def _consts():
    c = {}
    c["ident"] = np.eye(128, dtype=np.float32).astype(ml_dtypes.bfloat16)
    ida = np.eye(128, dtype=np.float32)
    idb = np.zeros((128, 128), np.float32)
    for p in range(64):
        idb[p, 64 + p] = 1.0
        idb[64 + p, p] = -1.0
    half = 32
    inv_freq = (np.float32(10000.0) ** (-np.arange(half, dtype=np.float32) / np.float32(half))).astype(np.float32)
    pos = np.arange(S_LEN, dtype=np.float32)
    ang = (pos[None, :] * inv_freq[:, None]).astype(np.float32)
    c["cosT"] = np.tile(np.cos(ang).astype(np.float32), (4, 1))
    c["sinT"] = np.tile(np.sin(ang).astype(np.float32), (4, 1))
    cm = np.zeros((128, 2, 256), np.float32)
    di = np.zeros((128, 2, 256), np.float32)
    for mh in range(2):
        for sl in range(8):
            s_ = mh * 8 + sl
            for hi in range(16):
                p = sl * 16 + hi
                for t in range(16):
                    if t >= s_:
                        cm[p, mh, t * 16:(t + 1) * 16] = 1.0
                di[p, mh, s_ * 16 + hi] = 1.0
    kn = np.tile(-np.arange(16, dtype=np.float32)[None, :], (128, 1))
    kp = np.tile(np.arange(32, dtype=np.float32)[None, :], (128, 1))
    cidx = np.tile(np.arange(128, dtype=np.float32)[None, :], (128, 1))
    sgn = np.ones((128, 1), np.float32)
    sgn[64:] = -1.0
    c["ssmc"] = np.concatenate([ida, idb, cm.reshape(128, 512), di.reshape(128, 512), kn, kp, cidx, sgn],
                               axis=1).astype(np.float32)
    return c


SSMC_W = 128 + 128 + 512 + 512 + 16 + 32 + 128 + 1


def _prod(xs):
    r = 1
    for v in xs:
        r *= int(v)
    return r


AW = 43464


def build_program(dbg=None, opts=None):
    opts = opts or {}
    upto = opts.get("upto", "D")
    nc = bass.Bass("TRN2", target_bir_lowering=False)

    def din(name, shape, dt=F32):
        return nc.dram_tensor(name, list(shape), dt, kind="ExternalInput").ap()

    x_d = din("x", [S_LEN, D])
    w_in_d = din("w_in", [D, 7168])
    ident_d = din("ident", [128, 128], BF16)
    cos_d = din("cosT", [128, S_LEN])
    sin_d = din("sinT", [128, S_LEN])
    ssmc_d = din("ssmc", [128, SSMC_W])
    ssmp_d = din("ssmp", [128, 4 * 32 + 4 * 512])
    bgate_d = din("bgate", [128, 16])
    lnt_d = din("lnt", [128, 4 * D])
    w_glu_d = din("w_glu", [512, 1024])
    w_ab_d = din("w_attn_br", [512, 1024])
    w_sb_d = din("w_ssm_br", [512, 1024])
    w_out_d = din("w_out", [D, D])
    w_fg_d = din("w_ff_gate", [D, DFF])
    w_fu_d = din("w_ff_up", [D, DFF])
    w_fd_d = din("w_ff_down", [DFF, D])
    out_d = nc.dram_tensor("out", [S_LEN, D], F32, kind="ExternalOutput").ap()
    dbg_d = {}
    if dbg:
        for name, (shape, dt) in dbg.items():
            dbg_d[name] = nc.dram_tensor("dbg_" + name, list(shape), dt, kind="ExternalOutput").ap()

    S = Sched(nc)
    es = ExitStack()

    def sb(name, shape, dt):
        return es.enter_context(nc.sbuf_tensor("s_" + name, list(shape), dt))

    ARENA = sb("arena", [128, AW], F32)

    def carve(off, shape, dt):
        n = _prod(shape[1:])
        if dt == BF16:
            assert n % 2 == 0
            ap = ARENA[:, off:off + n // 2].bitcast(BF16)
            words = n // 2
        elif dt == I32:
            ap = ARENA[:, off:off + n].bitcast(I32)
            words = n
        else:
            ap = ARENA[:, off:off + n]
            words = n
        assert off + words <= AW, (off, words)
        if len(shape) > 2:
            names = ["d%d" % i for i in range(len(shape) - 1)]
            pat = "p (%s) -> p %s" % (" ".join(names), " ".join(names))
            ap = ap.rearrange(pat, **{nm: int(v) for nm, v in zip(names[:-1], shape[1:-1])})
        return ap

    xT = sb("xT", [128, NKT, S_LEN], BF16)
    xT_b = S.buf("xT")
    ident = sb("ident", [128, 128], BF16)
    ident_b = S.buf("ident")
    ssmc = sb("ssmc", [128, SSMC_W], F32)
    ssmc_b = S.buf("ssmc")
    bgate = sb("bgate", [128, 16], F32)
    IDA = ssmc[:, 0:128]
    IDB = ssmc[:, 128:256]
    CM = ssmc[:, 256:768].rearrange("p (m c) -> p m c", m=2)
    DI = ssmc[:, 768:1280].rearrange("p (m c) -> p m c", m=2)
    KN = ssmc[:, 1280:1296]
    KP = ssmc[:, 1296:1328]
    CIDX = ssmc[:, 1328:1456]
    SGN = ssmc[:, 1456:1457]

    PSALL = es.enter_context(nc.psum_tensor("p_all", [128, 8, 512], F32))
    PS = [PSALL[:, i, :] for i in range(8)]
    PS_b = [S.buf("ps%d" % i, excl=True) for i in range(8)]

    out_toks = []
    S.dma("sp", lambda e: e.dma_start(out=ident[:], in_=ident_d), ident_b, writes=[ident_b])
    S.dma("sp", lambda e: e.dma_start(out=ssmc[:], in_=ssmc_d), ssmc_b, writes=[ssmc_b])
    S.dma("sp", lambda e: e.dma_start(out=bgate[:], in_=bgate_d), ssmc_b, writes=[ssmc_b])

    attnT = carve(0, [128, 4, S_LEN], BF16)
    attnT_b = S.buf("attnT")
    YgT = carve(4096, [128, 4, S_LEN], BF16)
    YgT_b = S.buf("YgT")

    NXS = 8
    xin = [carve(36000 + 512 * i, [128, D], BF16) for i in range(NXS)]
    xin_b = [S.buf("xin%d" % i) for i in range(NXS)]
    x_tiles = x_d.rearrange("(j p) d -> j p d", p=128)
    for j in range(16):
        sl = j % NXS
        S.dma("pool", lambda e, j=j, sl=sl: e.dma_start(out=xin[sl], in_=x_tiles[j]),
              xin_b[sl], writes=[xin_b[sl]])
        bank = 6 + (j % 2)
        pst = PS[bank].bitcast(BF16)
        fns = []
        for kt in range(NKT):
            fns.append(lambda e, kt=kt, sl=sl, pst=pst: e.transpose(
                pst[:, kt * 128:(kt + 1) * 128], xin[sl][:, kt * 128:(kt + 1) * 128], ident[:]))
        S.group("pe", fns, reads=[xin_b[sl], ident_b], writes=[PS_b[bank]])
        src = pst.rearrange("p (k t) -> p k t", k=NKT)
        dst = xT[:, :, j * 128:(j + 1) * 128]
        if j % 2 == 0:
            S.op("act", lambda e, src=src, dst=dst: e.activation(out=dst, in_=src, func=AF.Copy),
                 reads=[PS_b[bank]], writes=[xT_b])
        else:
            S.op("dve", lambda e, src=src, dst=dst: e.tensor_copy(out=dst, in_=src),
                 reads=[PS_b[bank]], writes=[xT_b])

    w_in_k = w_in_d.rearrange("(kt p) c -> p kt c", p=128)

    pref_c = {"Wu_b": S.buf("Wu"), "prm_b": S.buf("prm"), "done": False}

    def prefetch_c(extra_bufs=()):
        if pref_c["done"]:
            return
        pref_c["done"] = True
        Wu_ = carve(16384, [128, NKT, 512], BF16)
        PRM_ = carve(18432, [128, 4 * 32 + 4 * 512], F32)
        S.dma("pool", lambda e: e.dma_start(out=Wu_, in_=w_in_k[:, :, 4608:5120]), pref_c["Wu_b"],
              writes=[pref_c["Wu_b"]] + list(extra_bufs))
        S.dma("sp", lambda e: e.dma_start(out=PRM_, in_=ssmp_d), pref_c["prm_b"],
              writes=[pref_c["prm_b"]] + list(extra_bufs))

    def stage_b():
        cosT = carve(4096, [128, S_LEN], F32)
        sinT = carve(6144, [128, S_LEN], F32)
        tab_b = S.buf("ropetab")
        S.dma("sp", lambda e: e.dma_start(out=cosT, in_=cos_d), tab_b, writes=[tab_b])
        S.dma("sp", lambda e: e.dma_start(out=sinT, in_=sin_d), tab_b, writes=[tab_b])
        ACC = carve(8192, [128, 4, S_LEN], F32)
        ACC_b = S.buf("ACC")
        RT = [[carve(16384 + (i * 4 + k) * 512, [128, 512], F32) for k in range(4)] for i in range(2)]
        RT_b = [[S.buf("rt") for k in range(4)] for i in range(2)]
        WB = [carve(20480 + 3072 * i, [128, NKT, 768], BF16) for i in range(2)]
        WB_b = [S.buf("wb%d" % i) for i in range(2)]
        Qg = carve(26624, [128, 2, S_LEN], BF16)
        Kg = carve(28672, [128, 2, S_LEN], BF16)
        Q_b, K_b = S.buf("Qg"), S.buf("Kg")
        Vaug = carve(30720, [128, 16, 4, 128], BF16)
        V_b = S.buf("Vaug")
        PB = [carve(34816 + 512 * i, [128, 4, 2, 128], BF16) for i in range(2)]
        PB_b = [S.buf("pb") for i in range(2)]

        S.op("pool", lambda e: e.memset(Vaug[:, :, :, 64:128], 1.0), writes=[V_b])

        WST = carve(40200, [128, 2, NKT, 256], BF16)
        WST_b = S.buf("wst")

        def load_weights(it):
            hq, g = divmod(it, 3)
            sl = it % 2
            wb = WB[sl]
            for qk in range(2):
                base = qk * 1536 + g * 512 + hq * 256
                S.dma("pool", lambda e, base=base, qk=qk: e.dma_start(out=WST[:, qk, :, :], in_=w_in_k[:, :, base:base + 256]),
                      WST_b, writes=[WST_b])
            base = 3072 + g * 512 + hq * 256
            S.dma("pool", lambda e, base=base, wb=wb: e.dma_start(out=wb[:, :, 512:768], in_=w_in_k[:, :, base:base + 256]),
                  WB_b[sl], writes=[WB_b[sl]])
            for qk in range(2):
                src = WST[:, qk, :, :].rearrange("p k (h t i) -> p k t h i", h=4, t=2)
                dst = wb[:, :, qk * 256:(qk + 1) * 256].rearrange("p k (t h i) -> p k t h i", t=2, h=4)
                for half in range(2):
                    S.op("act", lambda e, src=src, dst=dst, half=half: e.activation(
                        out=dst[:, :, half, :, :], in_=src[:, :, half, :, :], func=AF.Copy),
                        reads=[WST_b], writes=[WB_b[sl]])

        ZT0 = carve(40200, [128, S_LEN], F32)
        ZT1 = carve(36000, [128, 4, S_LEN], BF16)
        ZT_b = [S.buf("zt") for i in range(2)]
        pending = [None]

        def emit_norm_act(hq_):
            for hh in range(4):
                S.op("act", lambda e, hh=hh: e.activation(out=ZT0[64:128, :], in_=ACC[64:128, hh, :], func=AF.Ln),
                     reads=[ACC_b], writes=[ZT_b[0], WST_b])
                S.op("act", lambda e, hh=hh: e.activation(out=ZT1[0:64, hh, :], in_=ZT0[64:128, :], func=AF.Exp, scale=-1.0),
                     reads=[ZT_b[0], WST_b], writes=[ZT_b[1]])

        def emit_norm(hq_):
            for hh in range(4):
                h_glob = hq_ * 4 + hh
                pbase = 64 * (h_glob % 2)
                dst = attnT[pbase:pbase + 64, h_glob // 2, :]
                S.op("dve", lambda e, hh=hh, dst=dst: e.tensor_tensor(out=dst, in0=ACC[0:64, hh, :], in1=ZT1[0:64, hh, :], op=ALU.mult),
                     reads=[ACC_b, ZT_b[1]], writes=[attnT_b])

        load_weights(0)
        chunk_ctr = [0]
        for it in range(6):
            hq, g = divmod(it, 3)
            d = DIL[g]
            sl = it % 2
            wb = WB[sl]
            if it + 1 < 6:
                load_weights(it + 1)
            for qk in range(2):
                dstT, dst_b = (Qg, Q_b) if qk == 0 else (Kg, K_b)
                for tc in range(4):
                    cc = chunk_ctr[0]
                    chunk_ctr[0] += 1
                    b1, b2 = (0, 1) if cc % 2 == 0 else (2, 3)
                    for half, bank in ((0, b1), (1, b2)):
                        off = (qk * 2 + half) * 128
                        fns = []
                        for kt in range(NKT):
                            fns.append(lambda e, kt=kt, off=off, bank=bank, tc=tc, wb=wb: e.matmul(
                                PS[bank], lhsT=wb[:, kt, off:off + 128], rhs=xT[:, kt, tc * 512:(tc + 1) * 512],
                                start=(kt == 0), stop=(kt == NKT - 1)))
                        S.group("pe", fns, reads=[WB_b[sl], xT_b], writes=[PS_b[bank]])
                    rs = cc % 2
                    ra, rb, rc, rd = RT[rs]
                    ra_b, rb_b, rc_b, rd_b = RT_b[rs]
                    cs = cosT[:, tc * 512:(tc + 1) * 512]
                    sn = sinT[:, tc * 512:(tc + 1) * 512]
                    dd = d if qk == 0 else 1
                    npc = 512 // dd

                    def pv_(ap, dd=dd):
                        return ap.rearrange("p (m r) -> p r m", r=dd)

                    def cv_(ap, dd=dd):
                        return ap.rearrange("p (r m) -> p r m", r=dd)
                    S.op("dve", lambda e, ra=ra, b1=b1, cs=cs, pv_=pv_, cv_=cv_: e.tensor_tensor(out=cv_(ra), in0=pv_(PS[b1]), in1=pv_(cs), op=ALU.mult),
                         reads=[PS_b[b1], tab_b], writes=[ra_b])
                    S.op("dve", lambda e, rc=rc, b1=b1, sn=sn, pv_=pv_, cv_=cv_: e.tensor_tensor(out=cv_(rc), in0=pv_(PS[b1]), in1=pv_(sn), op=ALU.mult),
                         reads=[PS_b[b1], tab_b], writes=[rc_b])
                    S.op("dve", lambda e, rb=rb, b2=b2, sn=sn, pv_=pv_, cv_=cv_: e.tensor_tensor(out=cv_(rb), in0=pv_(PS[b2]), in1=pv_(sn), op=ALU.mult),
                         reads=[PS_b[b2], tab_b], writes=[rb_b])
                    S.op("dve", lambda e, rd=rd, b2=b2, cs=cs, pv_=pv_, cv_=cv_: e.tensor_tensor(out=cv_(rd), in0=pv_(PS[b2]), in1=pv_(cs), op=ALU.mult),
                         reads=[PS_b[b2], tab_b], writes=[rd_b])
                    for half, (i0, i1, op) in enumerate(((ra, rb, ALU.subtract), (rc, rd, ALU.add))):
                        i0b, i1b = ((ra_b, rb_b), (rc_b, rd_b))[half]
                        dview = dstT[:, half, :].rearrange("p (r m) -> p r m", r=dd)[:, :, npc * tc:npc * (tc + 1)]
                        S.op("dve" if half == 0 else "pool", lambda e, dview=dview, i0=i0, i1=i1, op=op, cv_=cv_: e.tensor_tensor(
                            out=dview, in0=cv_(i0), in1=cv_(i1), op=op), reads=[i0b, i1b], writes=[dst_b])
            nb = S_LEN // (128 * d)
            for bp in range(8):
                cc = chunk_ctr[0]
                chunk_ctr[0] += 1
                bank = cc % 4
                for sub in range(2):
                    blk = bp * 2 + sub
                    r, b = divmod(blk, nb)
                    fns = []
                    for kt in range(NKT):
                        lt = xT[:, kt, b * 128 * d:(b + 1) * 128 * d].rearrange("p (a r) -> p a r", r=d)[:, :, r]
                        fns.append(lambda e, kt=kt, lt=lt, bank=bank, sub=sub, wb=wb: e.matmul(
                            PS[bank][:, sub * 256:(sub + 1) * 256], lhsT=lt, rhs=wb[:, kt, 512:768],
                            start=(kt == 0), stop=(kt == NKT - 1)))
                    S.group("pe", fns, reads=[WB_b[sl], xT_b], writes=[PS_b[bank]])
                src = PS[bank].rearrange("p (s h c) -> p s h c", s=2, h=4)
                dst = Vaug[:, bp * 2:bp * 2 + 2, :, 0:64]
                S.op("act", lambda e, src=src, dst=dst: e.activation(out=dst, in_=src, func=AF.Copy),
                     reads=[PS_b[bank]], writes=[V_b])
            if pending[0] is not None:
                emit_norm(pending[0])
                pending[0] = None
            if it == 5:
                prefetch_c([b_ for row in RT_b for b_ in row] + [WB_b[0]])
            blocks = []
            for blk in range(16):
                r, b = divmod(blk, nb)
                kts = ([("prev", blk - 1)] if b > 0 else []) + [("cur", blk)]
                blocks.append((blk, r, b, kts))

            def kview(hh, half, kblk, d=d, nb=nb):
                kr, kb = divmod(kblk, nb)
                return Kg[32 * hh:32 * hh + 32, half, kb * 128 * d:(kb + 1) * 128 * d].rearrange("p (a r) -> p a r", r=d)[:, :, kr]

            def emit_qk(bi):
                blk, r, b, kts = blocks[bi]
                par = bi % 2
                nk = len(kts)
                for hp in range(2):
                    fns = []
                    for hh in (2 * hp, 2 * hp + 1):
                        for ki, (kind, kblk) in enumerate(kts):
                            for half in range(2):
                                fns.append(lambda e, hh=hh, half=half, kblk=kblk, blk=blk, ki=ki, kview=kview: e.matmul(
                                    PSALL[:, hh, ki * 128:(ki + 1) * 128],
                                    lhsT=kview(hh, half, kblk),
                                    rhs=Qg[32 * hh:32 * hh + 32, half, blk * 128:(blk + 1) * 128],
                                    start=(half == 0), stop=(half == 1), tile_position=(32 * hh, 0)))
                    S.group("pe", fns, reads=[Q_b, K_b], writes=PS_b[2 * hp:2 * hp + 2])
                    S.op("act", lambda e, par=par, nk=nk, hp=hp: e.activation(
                        out=PB[par][:, 2 * hp:2 * hp + 2, 0:nk, :],
                        in_=PSALL[:, 2 * hp:2 * hp + 2, 0:nk * 128].rearrange("p h (k q) -> p h k q", k=nk),
                        func=AF.Exp, scale=0.125), reads=PS_b[2 * hp:2 * hp + 2], writes=[PB_b[par]])
                for ki, (kind, kblk) in enumerate(kts):
                    pv = PB[par][:, :, ki, :]
                    if kind == "cur":
                        S.op("pool", lambda e, pv=pv: e.affine_select(
                            out=pv, in_=pv, pattern=[[0, 4], [1, 128]], base=0, channel_multiplier=-1,
                            compare_op=ALU.is_ge, fill=0.0), writes=[PB_b[par]])
                    else:
                        S.op("pool", lambda e, pv=pv: e.affine_select(
                            out=pv, in_=pv, pattern=[[0, 4], [-1, 128]], base=0, channel_multiplier=1,
                            compare_op=ALU.is_ge, fill=0.0), writes=[PB_b[par]])

            def emit_pv(bi):
                blk, r, b, kts = blocks[bi]
                obank = 4 + (bi % 2)
                fns = []
                par = bi % 2
                nk = len(kts)
                for hh in range(4):
                    for ki, (kind, kblk) in enumerate(kts):
                        fns.append(lambda e, hh=hh, kblk=kblk, par=par, ki=ki, obank=obank, nk=nk: e.matmul(
                            PS[obank][:, hh * 128:(hh + 1) * 128], lhsT=Vaug[:, kblk, hh, :],
                            rhs=PB[par][:, hh, ki, :], start=(ki == 0), stop=(ki == nk - 1)))
                S.group("pe", fns, reads=[V_b, PB_b[par]], writes=[PS_b[obank]])
                if d == 1:
                    dst = ACC.rearrange("p h (r m) -> p h m r", r=16)[:, :, 8 * b:8 * b + 8, :]
                    src = PS[obank].rearrange("p (h m r) -> p h m r", h=4, r=16)
                elif d == 4:
                    dst = ACC.rearrange("p h (q s m) -> p h m q s", q=4, s=4)[:, :, 32 * b:32 * b + 32, :, r]
                    src = PS[obank].rearrange("p (h m q) -> p h m q", h=4, q=4)
                else:
                    dst = ACC[:, :, r * 128:(r + 1) * 128]
                    src = PS[obank].rearrange("p (h m) -> p h m", h=4)
                if g == 0:
                    S.op("dve", lambda e, dst=dst, src=src: e.tensor_copy(out=dst, in_=src),
                         reads=[PS_b[obank]], writes=[ACC_b])
                else:
                    S.op("dve", lambda e, dst=dst, src=src: e.tensor_tensor(out=dst, in0=src, in1=dst, op=ALU.add),
                         reads=[PS_b[obank]], writes=[ACC_b])

            emit_qk(0)
            for bi in range(16):
                if bi + 1 < 16:
                    emit_qk(bi + 1)
                emit_pv(bi)

            if g == 2:
                emit_norm_act(hq)
                pending[0] = hq
        if pending[0] is not None:
            emit_norm(pending[0])
            pending[0] = None

    if upto in "BCD" and not opts.get("skip_b", False):
        stage_b()
        S.barrier()

    def stage_c():
        base = [8192]

        def alloc(shape, dt):
            n = _prod(shape[1:])
            words = n // 2 if dt == BF16 else n
            ap = carve(base[0], shape, dt)
            base[0] += words
            return ap

        U_sb = alloc([128, 32, 16, 16], BF16)
        U_b = S.buf("U")
        X1 = alloc([128, 32, 16], F32)
        X2 = alloc([128, 32, 16], F32)
        Y1 = alloc([128, 32, 16], F32)
        Y2 = alloc([128, 32, 16], F32)
        LS = alloc([128, 8, 256], F32)
        assert base[0] == 16384
        Wu = alloc([128, NKT, 512], BF16)
        PRM = alloc([128, 4 * 32 + 4 * 512], F32)
        Wu_b, prm_b = pref_c["Wu_b"], pref_c["prm_b"]
        prefetch_c()
        AR, AI, LDT, DCOL = (PRM[:, 32 * i:32 * (i + 1)] for i in range(4))
        BA, BB, CA, CB = (PRM[:, 128 + 512 * i:128 + 512 * (i + 1)].rearrange("p (g h) -> p g h", g=32) for i in range(4))
        SC = alloc([128, 26, 32], F32)
        XR = alloc([128, 8, 256], F32)
        PNr = alloc([128, 32, 16], F32)
        PNi = alloc([128, 32, 16], F32)
        PPr = alloc([128, 32, 32], F32)
        PPi = alloc([128, 32, 32], F32)
        ta_off = base[0]
        TA = alloc([128, 8, 256], F32)
        XT1 = carve(ta_off, [128, 32, 16], F32)
        TB = alloc([128, 8, 256], F32)
        Wintra = alloc([128, 8, 2, 256], BF16)
        W1T = alloc([128, 8, 2, 128], BF16)
        WdS = alloc([128, 8, 256], BF16)
        UL = alloc([128, 8, 2, 128], BF16)
        BM1 = alloc([128, 1024], F32)
        BM2 = alloc([128, 1024], F32)
        BSW = alloc([128, 1024], F32)
        BGS = alloc([128, 1024], F32)
        EC = alloc([128, 1024], F32)
        ES2 = alloc([128, 1024], F32)
        HP = alloc([128, 8, 128], BF16)
        Ytok = alloc([128, 16, 128], BF16)
        KI = BSW.bitcast(I32)
        assert base[0] <= AW, base[0]
        g_b = S.buf("gen")
        q_b = S.buf("qgen")
        wts_b = S.buf("wts")
        UL_b = S.buf("UL")
        scan_b = S.buf("scan")
        tabs_b = S.buf("tabs")
        HP_b = S.buf("HP")
        Yt_b = S.buf("Ytok")

        def sc(i):
            return SC[:, i, :]

        def dve_tt(out, in0, in1, op, rd, wr):
            S.op("dve", lambda e: e.tensor_tensor(out=out, in0=in0, in1=in1, op=op), reads=rd, writes=wr)

        def dve_ts(out, in0, s1, op0, s2=None, op1=None, rd=(), wr=()):
            if op1 is None:
                S.op("dve", lambda e: e.tensor_scalar(out=out, in0=in0, scalar1=s1, scalar2=None, op0=op0), reads=rd, writes=wr)
            else:
                S.op("dve", lambda e: e.tensor_scalar(out=out, in0=in0, scalar1=s1, scalar2=s2, op0=op0, op1=op1), reads=rd, writes=wr)

        def act(out, in_, func, rd, wr, scale=1.0, bias=0.0):
            S.op("act", lambda e: e.activation(out=out, in_=in_, func=func, scale=scale, bias=bias), reads=rd, writes=wr)

        def sincos(x, n_shape_view, sin_out, cos_out, tmp_u, tmp_k, tmp_i, rd, wr):
            dve_ts(tmp_u, x, 1.0 / TWO_PI, ALU.mult, rd=rd, wr=wr)
            S.op("dve", lambda e: e.tensor_copy(out=tmp_i, in_=tmp_u), reads=rd, writes=wr)
            S.op("dve", lambda e: e.tensor_copy(out=tmp_k, in_=tmp_i), reads=rd, writes=wr)
            dve_tt(tmp_u, tmp_u, tmp_k, ALU.subtract, rd, wr)
            if sin_out is not None:
                act(sin_out, tmp_u, AF.Sin, rd, wr, scale=TWO_PI, bias=0.0)
            if cos_out is not None:
                dve_ts(tmp_k, tmp_u, -1.0, ALU.mult, rd=rd, wr=wr)
                dve_tt(tmp_k, tmp_k, tmp_u, ALU.max, rd, wr)
                act(cos_out, tmp_k, AF.Sin, rd, wr, scale=-TWO_PI, bias=math.pi / 2)

        G = [g_b, prm_b, ssmc_b]
        (iDT, iARDT, iTH, iMAG, iSIN, iCOS, iABR, iABI, iNRE, iT1, iT2, iDEN, iCR, iCI,
         iRHO16, iTH16, iPH16, iU, iK) = range(19)
        KIs = KI[:, 0:32]
        act(sc(iDT), LDT, AF.Exp, G, [g_b])
        dve_tt(sc(iARDT), AR, sc(iDT), ALU.mult, G, [g_b])
        dve_tt(sc(iTH), AI, sc(iDT), ALU.mult, G, [g_b])
        act(sc(iMAG), sc(iARDT), AF.Exp, G, [g_b])
        sincos(sc(iTH), None, sc(iSIN), sc(iCOS), sc(iU), sc(iK), KIs, G, [g_b])
        dve_tt(sc(iABR), sc(iMAG), sc(iCOS), ALU.mult, G, [g_b])
        dve_tt(sc(iABI), sc(iMAG), sc(iSIN), ALU.mult, G, [g_b])
        dve_ts(sc(iNRE), sc(iABR), -1.0, ALU.add, rd=G, wr=[g_b])
        dve_tt(sc(iT1), AR, AR, ALU.mult, G, [g_b])
        dve_tt(sc(iT2), AI, AI, ALU.mult, G, [g_b])
        dve_tt(sc(iDEN), sc(iT1), sc(iT2), ALU.add, G, [g_b])
        S.op("dve", lambda e: e.reciprocal(out=sc(iDEN), in_=sc(iDEN)), reads=G, writes=[g_b])
        dve_tt(sc(iT1), sc(iNRE), AR, ALU.mult, G, [g_b])
        dve_tt(sc(iT2), sc(iABI), AI, ALU.mult, G, [g_b])
        dve_tt(sc(iT1), sc(iT1), sc(iT2), ALU.add, G, [g_b])
        dve_tt(sc(iCR), sc(iT1), sc(iDEN), ALU.mult, G, [g_b])
        dve_tt(sc(iT1), sc(iABI), AR, ALU.mult, G, [g_b])
        dve_tt(sc(iT2), sc(iNRE), AI, ALU.mult, G, [g_b])
        dve_tt(sc(iT1), sc(iT1), sc(iT2), ALU.subtract, G, [g_b])
        dve_tt(sc(iCI), sc(iT1), sc(iDEN), ALU.mult, G, [g_b])
        act(sc(iRHO16), sc(iARDT), AF.Exp, G, [g_b], scale=16.0)
        dve_ts(sc(iTH16), sc(iTH), 16.0 / TWO_PI, ALU.mult, rd=G, wr=[g_b])
        S.op("dve", lambda e: e.tensor_copy(out=KIs, in_=sc(iTH16)), reads=G + [scan_b], writes=[g_b, scan_b])
        S.op("dve", lambda e: e.tensor_copy(out=sc(iK), in_=KIs), reads=G + [scan_b], writes=[g_b, scan_b])
        dve_tt(sc(iTH16), sc(iTH16), sc(iK), ALU.subtract, G, [g_b])
        dve_ts(sc(iPH16), sc(iTH16), TWO_PI, ALU.mult, rd=G, wr=[g_b])
        dve_ts(BB[0:64], BB[0:64], -1.0, ALU.mult, rd=G, wr=[g_b])
        CRb = SC[:, iCR, :].rearrange("p (g o) -> p g o", o=1).to_broadcast([128, 32, 16])
        CIb = SC[:, iCI, :].rearrange("p (g o) -> p g o", o=1).to_broadcast([128, 32, 16])
        dve_tt(X1, BA, CRb, ALU.mult, G, [g_b])
        dve_tt(XT1, BB, CIb, ALU.mult, G, [g_b])
        dve_tt(X1, X1, XT1, ALU.add, G, [g_b])
        dve_tt(X2, BB, CRb, ALU.mult, G, [g_b])
        dve_tt(XT1, BA, CIb, ALU.mult, G, [g_b])
        dve_tt(X2, X2, XT1, ALU.subtract, G, [g_b])
        dve_ts(Y1, CA, 1.0, ALU.mult, rd=G, wr=[g_b])
        dve_ts(Y1[64:128], CA[64:128], -1.0, ALU.mult, rd=G, wr=[g_b])
        dve_ts(Y2, CB, -1.0, ALU.mult, rd=G, wr=[g_b])

        ardt_a = SC[:, iARDT, :].rearrange("p (g o) -> p g o", o=1)
        th_a = SC[:, iTH, :].rearrange("p (g o) -> p g o", o=1)
        for (KT, n, Pr, Pi) in ((KN, 16, PNr, PNi), (KP, 32, PPr, PPi)):
            kb = KT.rearrange("p (o k) -> p o k", o=1).to_broadcast([128, 32, n])
            ta = TA.rearrange("p g c -> p (g c)")[:, 0:32 * n].rearrange("p (g k) -> p g k", g=32)
            tb = TB.rearrange("p g c -> p (g c)")[:, 0:32 * n].rearrange("p (g k) -> p g k", g=32)
            tu = BM1[:, 0:32 * n].rearrange("p (g k) -> p g k", g=32)
            tk = BM2[:, 0:32 * n].rearrange("p (g k) -> p g k", g=32)
            ti = KI[:, 0:32 * n].rearrange("p (g k) -> p g k", g=32)
            GG = G + [scan_b]
            dve_tt(ta, kb, ardt_a.to_broadcast([128, 32, n]), ALU.mult, GG, [g_b])
            act(ta, ta, AF.Exp, GG, [g_b])
            dve_tt(tb, kb, th_a.to_broadcast([128, 32, n]), ALU.mult, GG, [g_b])
            sincos(tb, None, Pi, Pr, tu, tk, ti, GG, [g_b, scan_b])
            dve_tt(Pr, Pr, ta, ALU.mult, GG, [g_b])
            dve_tt(Pi, Pi, ta, ALU.mult, GG, [g_b])

        for s_ in range(16):
            bank = s_ % 2
            fns = []
            for kt in range(NKT):
                lt = xT[:, kt, :].rearrange("p (c s) -> p c s", s=16)[:, :, s_]
                fns.append(lambda e, kt=kt, lt=lt, bank=bank: e.matmul(
                    PS[bank], lhsT=lt, rhs=Wu[:, kt, :], start=(kt == 0), stop=(kt == NKT - 1)))
            S.group("pe", fns, reads=[xT_b, Wu_b], writes=[PS_b[bank]])
            src = PS[bank].rearrange("p (g h) -> p g h", g=32)
            dst = U_sb[:, :, s_, :]
            S.op("act", lambda e, src=src, dst=dst: e.activation(out=dst, in_=src, func=AF.Copy),
                 reads=[PS_b[bank]], writes=[U_b])

        S.op("pool", lambda e: e.memset(HP[:, :, 0:1], 0.0), writes=[HP_b])

        ls_b, ta_b, tb_b = S.buf("ls"), S.buf("ta"), S.buf("tb")
        LS4 = LS.rearrange("p g (s h) -> p g s h", s=16)
        TA4 = TA.rearrange("p g (s h) -> p g s h", s=16)
        TB4 = TB.rearrange("p g (s h) -> p g s h", s=16)
        XR4 = XR.rearrange("p g (t h) -> p g t h", t=16)
        Gq = [g_b, q_b]

        def bc(ap, pat):
            return ap.rearrange(pat, o=1).to_broadcast([128, 8, 16, 16])

        def prod_dve(q_):
            gs_ = slice(8 * q_, 8 * q_ + 8)
            dve_tt(LS4, bc(PNr[:, gs_, :], "p g (s o) -> p g s o"), bc(X1[:, gs_, :], "p g (o h) -> p g o h"), ALU.mult, Gq, [ls_b])
            dve_tt(TA4, bc(PNi[:, gs_, :], "p g (s o) -> p g s o"), bc(X2[:, gs_, :], "p g (o h) -> p g o h"), ALU.mult, Gq, [ta_b])
            dve_tt(LS, LS, TA, ALU.add, [ta_b], [ls_b])
            dve_tt(XR4, bc(PPr[:, gs_, 0:16], "p g (t o) -> p g t o"), bc(Y1[:, gs_, :], "p g (o h) -> p g o h"), ALU.mult, Gq, [ls_b])
            dve_tt(TA4, bc(PPi[:, gs_, 0:16], "p g (t o) -> p g t o"), bc(Y2[:, gs_, :], "p g (o h) -> p g o h"), ALU.mult, Gq, [ta_b])
            dve_tt(XR, XR, TA, ALU.add, [ta_b], [ls_b])

        def prod_pool(q_):
            gs_ = slice(8 * q_, 8 * q_ + 8)
            a0_, a1_ = bc(PPr[:, gs_, 16:32], "p g (t o) -> p g t o"), bc(Y1[:, gs_, :], "p g (o h) -> p g o h")
            b0_, b1_ = bc(PPi[:, gs_, 16:32], "p g (t o) -> p g t o"), bc(Y2[:, gs_, :], "p g (o h) -> p g o h")
            S.op("pool", lambda e: e.tensor_tensor(out=TB4, in0=a0_, in1=a1_, op=ALU.mult), reads=Gq, writes=[tb_b])
            S.op("pool", lambda e: e.tensor_tensor(out=TA4, in0=b0_, in1=b1_, op=ALU.mult), reads=Gq, writes=[ta_b])
            S.op("pool", lambda e: e.tensor_tensor(out=WdS.rearrange("p g c -> p (g c)"), in0=TB.rearrange("p g c -> p (g c)"),
                                                   in1=TA.rearrange("p g c -> p (g c)"), op=ALU.add),
                 reads=[tb_b, ta_b], writes=[wts_b])

        def tables(q_):
            gs_ = slice(8 * q_, 8 * q_ + 8)
            T_ = [tabs_b, g_b, ssmc_b]
            ph_q = SC[:, iPH16, gs_].rearrange("p (g o) -> p g o", o=1).to_broadcast([128, 8, 128])
            cidb = CIDX.rearrange("p (o c) -> p o c", o=1).to_broadcast([128, 8, 128])
            TAf = TA.rearrange("p g c -> p (g c)")[:, 0:1024]
            TAi = TAf.bitcast(I32)
            EC3 = EC.rearrange("p (g c) -> p g c", g=8)
            TQ = T_ + [ta_b]
            dve_tt(EC3, cidb, ph_q, ALU.mult, T_, [tabs_b])
            dve_ts(EC, EC, 1.0 / TWO_PI, ALU.mult, rd=T_, wr=[tabs_b])
            S.op("dve", lambda e, TAi=TAi: e.tensor_copy(out=TAi, in_=EC), reads=TQ, writes=[tabs_b, ta_b])
            S.op("dve", lambda e, TAi=TAi: e.tensor_copy(out=ES2, in_=TAi), reads=TQ, writes=[tabs_b, ta_b])
            dve_tt(EC, EC, ES2, ALU.subtract, T_, [tabs_b])
            act(ES2, EC, AF.Sin, T_, [tabs_b], scale=TWO_PI, bias=0.0)
            dve_ts(TAf, EC, -1.0, ALU.mult, rd=TQ, wr=[tabs_b, ta_b])
            dve_tt(TAf, TAf, EC, ALU.max, TQ, [tabs_b, ta_b])
            S.op("act", lambda e, TAf=TAf: e.activation(out=EC, in_=TAf, func=AF.Sin, scale=-TWO_PI, bias=math.pi / 2),
                 reads=TQ, writes=[tabs_b, ta_b])
            dve_ts(ES2, ES2, SGN, ALU.mult, rd=T_, wr=[tabs_b])

        prod_dve(0)
        prod_pool(0)
        for q in range(4):
            gs = slice(8 * q, 8 * q + 8)
            Q = [ls_b, ssmc_b]
            for g8 in range(8):
                gg = 8 * q + g8
                bank = 2 + (g8 % 2)
                fns = []
                for mh in range(2):
                    fns.append(lambda e, g8=g8, mh=mh, bank=bank: e.matmul(
                        PS[bank][:, mh * 256:(mh + 1) * 256], lhsT=LS[:, g8, mh * 128:(mh + 1) * 128],
                        rhs=XR[:, g8, :], start=True, stop=True))
                S.group("pe", fns, reads=Q, writes=[PS_b[bank]])
                S.op("dve", lambda e, g8=g8, bank=bank: e.tensor_tensor(
                    out=BSW[:, 0:512], in0=PS[bank], in1=CM.rearrange("p m c -> p (m c)"), op=ALU.mult),
                    reads=[PS_b[bank], ssmc_b], writes=[scan_b])
                S.op("dve", lambda e, g8=g8, gg=gg: e.scalar_tensor_tensor(
                    out=Wintra[:, g8, :, :].rearrange("p m c -> p (m c)"), in0=DI.rearrange("p m c -> p (m c)"),
                    scalar=DCOL[:, gg:gg + 1], in1=BSW[:, 0:512], op0=ALU.mult, op1=ALU.add),
                    reads=[scan_b, ssmc_b, prm_b], writes=[wts_b])
                bank2 = 4 + (g8 % 2)
                fns = []
                for mh in range(2):
                    fns.append(lambda e, g8=g8, mh=mh, bank2=bank2: e.matmul(
                        PS[bank2][:, mh * 128:(mh + 1) * 128], lhsT=LS[:, g8, mh * 128:(mh + 1) * 128],
                        rhs=IDA, start=True, stop=True))
                S.group("pe", fns, reads=Q, writes=[PS_b[bank2]])
                S.op("act", lambda e, g8=g8, bank2=bank2: e.activation(
                    out=W1T[:, g8, :, :], in_=PS[bank2][:, 0:256].rearrange("p (m c) -> p m c", m=2), func=AF.Copy),
                    reads=[PS_b[bank2]], writes=[wts_b])
            for hb in range(2):
                bank = 6 + hb
                pst = PS[bank].bitcast(BF16)
                fns = []
                for k in range(8):
                    g8, mh = divmod(hb * 8 + k, 2)
                    gg = 8 * q + g8
                    src = U_sb[:, gg, mh * 8:(mh + 1) * 8, :].rearrange("p s h -> p (s h)")
                    fns.append(lambda e, k=k, src=src, pst=pst: e.transpose(pst[:, k * 128:(k + 1) * 128], src, ident[:]))
                S.group("pe", fns, reads=[U_b, ident_b], writes=[PS_b[bank]])
                dst = UL[:, hb * 4:(hb + 1) * 4, :, :]
                srcp = pst.rearrange("p (g m c) -> p g m c", g=4, m=2)
                S.op("act", lambda e, dst=dst, srcp=srcp: e.activation(out=dst, in_=srcp, func=AF.Copy),
                     reads=[PS_b[bank]], writes=[UL_b])
            if q + 1 < 4:
                prod_dve(q + 1)
            if q == 0:
                tables(0)
            for hb in range(2):
                fns = []
                for k in range(4):
                    g8 = hb * 4 + k
                    for mh in range(2):
                        fns.append(lambda e, g8=g8, mh=mh, k=k, hb=hb: e.matmul(
                            PS[hb][:, k * 128:(k + 1) * 128], lhsT=W1T[:, g8, mh, :], rhs=UL[:, g8, mh, :],
                            start=(mh == 0), stop=(mh == 1)))
                S.group("pe", fns, reads=[wts_b, UL_b], writes=[PS_b[hb]])
                cs_ = slice(hb * 512, (hb + 1) * 512)
                S.op("dve", lambda e, hb=hb, cs_=cs_: e.tensor_tensor(out=BM1[:, cs_], in0=PS[hb], in1=EC[:, cs_], op=ALU.mult),
                     reads=[PS_b[hb], tabs_b], writes=[scan_b])
                S.op("act", lambda e, hb=hb, cs_=cs_: e.activation(out=BSW[0:64, cs_], in_=PS[hb][64:128, :], func=AF.Copy),
                     reads=[PS_b[hb]], writes=[scan_b])
                S.op("act", lambda e, hb=hb, cs_=cs_: e.activation(out=BSW[64:128, cs_], in_=PS[hb][0:64, :], func=AF.Copy),
                     reads=[PS_b[hb]], writes=[scan_b])
            R_ = [scan_b, tabs_b]
            dve_tt(BM2, BSW, ES2, ALU.mult, R_, [scan_b])
            dve_tt(BM1, BM1, BM2, ALU.add, R_, [scan_b])
            for g8 in range(8):
                gg = 8 * q + g8
                S.op("dve", lambda e, g8=g8, gg=gg: e.tensor_tensor_scan(
                    out=BGS[:, g8 * 128:(g8 + 1) * 128], data0=SC[:, iRHO16, gg:gg + 1].to_broadcast([128, 128]),
                    data1=BM1[:, g8 * 128:(g8 + 1) * 128], initial=0.0, op0=ALU.mult, op1=ALU.add),
                    reads=R_ + [g_b], writes=[scan_b])
            S.op("act", lambda e: e.activation(out=BSW[0:64, :], in_=BGS[64:128, :], func=AF.Copy), reads=R_, writes=[scan_b])
            S.op("act", lambda e: e.activation(out=BSW[64:128, :], in_=BGS[0:64, :], func=AF.Copy), reads=R_, writes=[scan_b])
            dve_tt(BM1, BGS, EC, ALU.mult, R_, [scan_b])
            dve_tt(BM2, BSW, ES2, ALU.mult, R_, [scan_b])
            m3 = BM1.rearrange("p (g c) -> p g c", g=8)[:, :, 0:127]
            m4 = BM2.rearrange("p (g c) -> p g c", g=8)[:, :, 0:127]
            S.op("dve", lambda e, m3=m3, m4=m4: e.tensor_tensor(out=HP[:, :, 1:128], in0=m3, in1=m4, op=ALU.subtract),
                 reads=R_, writes=[HP_b])
            if q + 1 < 4:
                tables(q + 1)
            for gp in range(4):
                bank = 2 + (gp % 2)
                fns = []
                for k in range(2):
                    g8 = gp * 2 + k
                    osl = PS[bank][:, k * 256:(k + 1) * 256]
                    fns.append(lambda e, g8=g8, osl=osl: e.matmul(osl, lhsT=UL[:, g8, 0, :], rhs=Wintra[:, g8, 0, :], start=True, stop=False))
                    fns.append(lambda e, g8=g8, osl=osl: e.matmul(osl, lhsT=UL[:, g8, 1, :], rhs=Wintra[:, g8, 1, :], start=False, stop=False))
                    fns.append(lambda e, g8=g8, osl=osl: e.matmul(osl, lhsT=HP[:, g8, :], rhs=WdS[:, g8, :], start=False, stop=True))
                S.group("pe", fns, reads=[UL_b, wts_b, HP_b], writes=[PS_b[bank]])
                src = PS[bank].rearrange("p (k t h) -> p k t h", k=2, t=16)
                dst = Ytok.rearrange("p t (g h) -> p g t h", g=8)[:, gp * 2:gp * 2 + 2, :, :]
                S.op("act", lambda e, src=src, dst=dst: e.activation(out=dst, in_=src, func=AF.Gelu_apprx_tanh),
                     reads=[PS_b[bank]], writes=[Yt_b])
            for hb in range(2):
                bank = 6 + hb
                pst = PS[bank].bitcast(BF16)
                fns = []
                for k in range(8):
                    t = hb * 8 + k
                    fns.append(lambda e, k=k, t=t, pst=pst: e.transpose(pst[:, k * 128:(k + 1) * 128], Ytok[:, t, :], ident[:]))
                S.group("pe", fns, reads=[Yt_b, ident_b], writes=[PS_b[bank]])
                S.op("dve", lambda e, hb=hb, q=q, pst=pst: e.tensor_copy(out=YgT[:, q, hb * 1024:(hb + 1) * 1024], in_=pst),
                     reads=[PS_b[bank]], writes=[YgT_b])
            if q + 1 < 4:
                prod_pool(q + 1)

    if upto in "CD" and not opts.get("skip_c", False):
        stage_c()
        S.barrier()

    def cast_load(dst, src, b):
        S.dma("pool", lambda e: e.dma_start(out=dst, in_=src), b, writes=[b])

    def stage_d():
        YS = carve(8192, [128, 4, S_LEN], BF16)
        YS_b = S.buf("YS")
        Wglu = carve(12288, [128, 4, 1024], BF16)
        Wab = carve(14336, [128, 4, 1024], BF16)
        Wsb = carve(16384, [128, 4, 1024], BF16)
        MX = carve(18432, [128, 8, S_LEN], BF16)
        MX_b = S.buf("MX")
        GT2 = [[carve(32256 + 1024 * (2 * p_ + i), [128, S_LEN], BF16) for i in range(2)] for p_ in range(2)]
        GT2_b = [[S.buf("gt") for i in range(2)] for p_ in range(2)]
        WG = [carve(28672 + 1024 * i, [128, NKT, 2, 128], BF16) for i in range(2)]
        WG_b = [S.buf("wg") for i in range(2)]
        TM = [carve(30720 + 512 * i, [128, 512], F32) for i in range(3)] + [carve(26624 + 512 * i, [128, 512], F32) for i in range(4)]
        TM_b = [S.buf("tm") for i in range(7)]
        Wo = carve(36352, [128, 8, 1024], BF16)
        wbr_b = S.buf("wbr")
        Wo_b = S.buf("Wo")
        cast_load(Wglu, w_glu_d.rearrange("(k p) c -> p k c", p=128), wbr_b)
        cast_load(Wab, w_ab_d.rearrange("(k p) c -> p k c", p=128), wbr_b)
        cast_load(Wsb, w_sb_d.rearrange("(k p) c -> p k c", p=128), wbr_b)

        def load_wg(dt):
            sl = dt % 2
            for br in range(2):
                c0 = 5120 + br * 1024 + dt * 128
                S.dma("pool", lambda e, sl=sl, br=br, c0=c0: e.dma_start(out=WG[sl][:, :, br, :], in_=w_in_k[:, :, c0:c0 + 128]),
                      WG_b[sl], writes=[WG_b[sl]])
        load_wg(0)
        cast_load(Wo, w_out_d.rearrange("(k p) c -> p k c", p=128), Wo_b)

        for f in range(4):
            for tc in range(4):
                csl = slice(tc * 512, (tc + 1) * 512)
                for which, bank in ((0, 0 + 2 * (tc % 2)), (1, 1 + 2 * (tc % 2))):
                    fns = []
                    col = (f + 4 * which) * 128
                    for kt in range(4):
                        fns.append(lambda e, kt=kt, col=col, bank=bank, csl=csl: e.matmul(
                            PS[bank], lhsT=Wglu[:, kt, col:col + 128], rhs=YgT[:, kt, csl], start=(kt == 0), stop=(kt == 3)))
                    S.group("pe", fns, reads=[wbr_b, YgT_b], writes=[PS_b[bank]])
                ba, bb_ = 0 + 2 * (tc % 2), 1 + 2 * (tc % 2)
                ti = tc % 2
                S.op("act", lambda e, bb_=bb_, ti=ti: e.activation(out=TM[ti], in_=PS[bb_], func=AF.Sigmoid),
                     reads=[PS_b[bb_]], writes=[TM_b[ti]])
                S.op("dve", lambda e, ba=ba, ti=ti, f=f, csl=csl: e.tensor_tensor(out=YS[:, f, csl], in0=PS[ba], in1=TM[ti], op=ALU.mult),
                     reads=[PS_b[ba], TM_b[ti]], writes=[YS_b])

        for dt in range(8):
            sl = dt % 2
            GT, GT_b = GT2[dt % 2], GT2_b[dt % 2]
            if dt + 1 < 8:
                load_wg(dt + 1)
            for br in range(2):
                for tc in range(4):
                    bank = 4 + ((br * 4 + tc) % 2)
                    fns = []
                    for kt in range(NKT):
                        fns.append(lambda e, kt=kt, sl=sl, br=br, bank=bank, tc=tc: e.matmul(
                            PS[bank], lhsT=WG[sl][:, kt, br, :], rhs=xT[:, kt, tc * 512:(tc + 1) * 512],
                            start=(kt == 0), stop=(kt == NKT - 1)))
                    S.group("pe", fns, reads=[WG_b[sl], xT_b], writes=[PS_b[bank]])
                    dst = GT[br].rearrange("p (r m) -> p r m", r=16)[:, :, 32 * tc:32 * tc + 32]
                    src = PS[bank].rearrange("p (m r) -> p r m", r=16)
                    S.op("act", lambda e, dst=dst, src=src, br=br, dt=dt: e.activation(
                        out=dst, in_=src, func=AF.Sigmoid, bias=bgate[:, br * 8 + dt:br * 8 + dt + 1]),
                        reads=[PS_b[bank], ssmc_b], writes=[GT_b[br]])
            for tc in range(4):
                csl = slice(tc * 512, (tc + 1) * 512)
                ba, bb_ = 0 + 2 * (tc % 2), 1 + 2 * (tc % 2)
                fns = []
                for kt in range(4):
                    fns.append(lambda e, kt=kt, ba=ba, csl=csl, dt=dt: e.matmul(
                        PS[ba], lhsT=Wab[:, kt, dt * 128:(dt + 1) * 128], rhs=attnT[:, kt, csl], start=(kt == 0), stop=(kt == 3)))
                S.group("pe", fns, reads=[wbr_b, attnT_b], writes=[PS_b[ba]])
                fns = []
                for kt in range(4):
                    fns.append(lambda e, kt=kt, bb_=bb_, csl=csl, dt=dt: e.matmul(
                        PS[bb_], lhsT=Wsb[:, kt, dt * 128:(dt + 1) * 128], rhs=YS[:, kt, csl], start=(kt == 0), stop=(kt == 3)))
                S.group("pe", fns, reads=[wbr_b, YS_b], writes=[PS_b[bb_]])
                t0i, t1i = 3 + 2 * (tc % 2), 4 + 2 * (tc % 2)
                S.op("dve", lambda e, ba=ba, csl=csl, GT=GT, t0i=t0i: e.tensor_tensor(out=TM[t0i], in0=PS[ba], in1=GT[0][:, csl], op=ALU.mult),
                     reads=[PS_b[ba], GT_b[0]], writes=[TM_b[t0i]])
                S.op("dve", lambda e, bb_=bb_, csl=csl, GT=GT, t1i=t1i: e.tensor_tensor(out=TM[t1i], in0=PS[bb_], in1=GT[1][:, csl], op=ALU.mult),
                     reads=[PS_b[bb_], GT_b[1]], writes=[TM_b[t1i]])
                S.op("pool", lambda e, dt=dt, csl=csl, t0i=t0i, t1i=t1i: e.tensor_tensor(out=MX[:, dt, csl], in0=TM[t0i], in1=TM[t1i], op=ALU.add),
                     reads=[TM_b[t0i], TM_b[t1i]], writes=[MX_b])
        return MX, MX_b, Wo, Wo_b

    def stage_e(MX, MX_b, Wo, Wo_b):
        hscr = nc.dram_tensor("hscr", [S_LEN, D], F32, kind="Internal").ap()
        hs_b = [S.buf("hs%d" % j) for j in range(16)]
        LNT = carve(0, [128, 4, D], F32)
        lnt_b = S.buf("lnt")
        S.dma("sp", lambda e: e.dma_start(out=LNT, in_=lnt_d.rearrange("p (k d) -> p k d", k=4)), lnt_b, writes=[lnt_b])
        Wdn = carve(4096, [128, 22, D], BF16)
        Wdn_b = S.buf("Wdn")
        HT1 = [carve(15360 + 1024 * i, [128, D], F32) for i in range(3)] + [carve(34944, [128, D], F32)]
        HT1_b = [S.buf("ht1") for i in range(4)]
        HBs = [carve(32768 + 512 * i, [128, 512], F32).bitcast(BF16) for i in range(2)]
        HBs_b = [S.buf("hb") for i in range(2)]
        STs = [carve(33792 + 32 * i, [128, 32], F32) for i in range(4)]
        STs_b = [S.buf("stats") for i in range(4)]
        wf_offs = [26624 + 2048 * i for i in range(3)] + [18432 + 2048 * i for i in range(4)]
        NWS = len(wf_offs)
        WF = [carve(o, [128, NKT, 2, 256], BF16) for o in wf_offs]
        WF_b = [S.buf("wf") for i in range(NWS)]
        AT = carve(32768, [128, 22, 512], BF16)
        AT_b = S.buf("AT")
        PRE = [carve(38400 + 1024 * i, [128, D], F32) for i in range(2)]
        PRE_b = [S.buf("pre") for i in range(2)]
        XJ = [carve(40448 + 1024 * i, [128, D], F32) for i in range(2)] + [carve(33920, [128, D], F32)]
        XJ_b = [S.buf("xj") for i in range(3)]
        SG = carve(42496, [128, 512], F32)
        SG_b = S.buf("sg")
        hT, hT_b = xT, xT_b
        x_rows = x_d.rearrange("(m r) d -> r m d", r=16)
        o_rows = out_d.rearrange("(m r) d -> r m d", r=16)
        wfg = w_fg_d.rearrange("(k p) c -> p k c", p=128)
        wfu = w_fu_d.rearrange("(k p) c -> p k c", p=128)
        NLOAD = 44

        def load_wf(k):
            f0 = 2 * (k % 11)
            sl = k % NWS
            S.dma("pool", lambda e, sl=sl, f0=f0: e.dma_start(out=WF[sl][:, :, 0, :], in_=wfg[:, :, f0 * 128:(f0 + 2) * 128]),
                  WF_b[sl], writes=[WF_b[sl]])
            S.dma("pool", lambda e, sl=sl, f0=f0: e.dma_start(out=WF[sl][:, :, 1, :], in_=wfu[:, :, f0 * 128:(f0 + 2) * 128]),
                  WF_b[sl], writes=[WF_b[sl]])

        def layer_norm(src, dst, gi, src_b, dst_b, par, phases="abc"):
            ST = STs[par]
            st_b = STs_b[par]
            stats = ST[:, 0:12].rearrange("p (c s) -> p c s", c=2)
            if "a" in phases:
                for c in range(2):
                    S.op("dve", lambda e, c=c: e.bn_stats(out=stats[:, c, :], in_=src[:, c * 512:(c + 1) * 512]),
                         reads=[src_b], writes=[st_b])
                S.op("dve", lambda e: e.bn_aggr(out=ST[:, 12:14], in_=ST[:, 0:12]), reads=[st_b], writes=[st_b])
                S.op("dve", lambda e: e.tensor_scalar(out=ST[:, 14:15], in0=ST[:, 13:14], scalar1=LN_EPS, scalar2=None, op0=ALU.add),
                     reads=[st_b], writes=[st_b])
                S.op("act", lambda e: e.activation(out=ST[:, 14:15], in_=ST[:, 14:15], func=AF.Sqrt), reads=[st_b], writes=[st_b])
            if "b" in phases:
                S.op("dve", lambda e: e.reciprocal(out=ST[:, 15:16], in_=ST[:, 14:15]), reads=[st_b], writes=[st_b])
                S.op("dve", lambda e: e.scalar_tensor_tensor(out=ST[:, 16:17], in0=ST[:, 12:13], scalar=-1.0, in1=ST[:, 15:16],
                                                             op0=ALU.mult, op1=ALU.mult), reads=[st_b], writes=[st_b])
                S.op("act", lambda e: e.activation(out=dst, in_=src, func=AF.Identity, scale=ST[:, 15:16], bias=ST[:, 16:17]),
                     reads=[src_b, st_b], writes=[dst_b])
            if "c" in phases:
                S.op("dve", lambda e: e.tensor_tensor(out=dst, in0=dst, in1=LNT[:, gi, :], op=ALU.mult),
                     reads=[lnt_b], writes=[dst_b])
                S.op("pool", lambda e: e.tensor_tensor(out=dst, in0=dst, in1=LNT[:, gi + 1, :], op=ALU.add),
                     reads=[lnt_b], writes=[dst_b])

        nloaded = [0]
        for k in range(3):
            load_wf(k)
            nloaded[0] += 1
        S.dma("pool", lambda e: e.dma_start(out=Wdn, in_=w_fd_d.rearrange("(f p) c -> p f c", p=128)), Wdn_b, writes=[Wdn_b])

        def e1_f0(j):
            xs = j % 3
            if j == 0:
                for j2 in range(2):
                    S.dma("sp", lambda e, j2=j2: e.dma_start(out=XJ[j2 % 3], in_=x_rows[j2]), XJ_b[j2 % 3], writes=[XJ_b[j2 % 3]])
            if j + 2 < 16:
                S.dma("sp", lambda e, j=j: e.dma_start(out=XJ[(j + 2) % 3], in_=x_rows[j + 2]), XJ_b[(j + 2) % 3], writes=[XJ_b[(j + 2) % 3]])
            ht = HT1[j % 4]
            ht_b = HT1_b[j % 4]
            for nh in range(2):
                bank = 2 * (j % 3) + nh
                fns = []
                for dt in range(8):
                    fns.append(lambda e, dt=dt, j=j, nh=nh, bank=bank: e.matmul(
                        PS[bank], lhsT=MX[:, dt, j * 128:(j + 1) * 128], rhs=Wo[:, dt, nh * 512:(nh + 1) * 512],
                        start=(dt == 0), stop=(dt == 7)))
                S.group("pe", fns, reads=[MX_b, Wo_b], writes=[PS_b[bank]])
                S.op("dve", lambda e, nh=nh, bank=bank, ht=ht, xs=xs: e.scalar_tensor_tensor(
                    out=ht[:, nh * 512:(nh + 1) * 512], in0=XJ[xs][:, nh * 512:(nh + 1) * 512], scalar=ALPHA,
                    in1=PS[bank], op0=ALU.mult, op1=ALU.add), reads=[XJ_b[xs], PS_b[bank]], writes=[ht_b])
            layer_norm(ht, ht, 0, ht_b, ht_b, j % 4, phases="a")

        def e1_f1(j):
            layer_norm(HT1[j % 4], HT1[j % 4], 0, HT1_b[j % 4], HT1_b[j % 4], j % 4, phases="b")

        def e1_f2(j):
            ht = HT1[j % 4]
            ht_b = HT1_b[j % 4]
            layer_norm(ht, ht, 0, ht_b, ht_b, j % 4, phases="c")
            S.dma("sp", lambda e, j=j, ht=ht: e.dma_start(out=hscr[j * 128:(j + 1) * 128, :], in_=ht), hs_b[j],
                  reads=[ht_b], writes=[hs_b[j]])
            HB = HBs[j % 2]
            S.op("act", lambda e, ht=ht, HB=HB: e.activation(out=HB, in_=ht, func=AF.Copy), reads=[ht_b], writes=[HBs_b[j % 2]])

        def e1_back(j):
            HB = HBs[j % 2]
            bank = 6 + (j % 2)
            pst = PS[bank].bitcast(BF16)
            fns = []
            for kt in range(NKT):
                fns.append(lambda e, kt=kt, pst=pst, HB=HB: e.transpose(pst[:, kt * 128:(kt + 1) * 128], HB[:, kt * 128:(kt + 1) * 128], ident[:]))
            S.group("pe", fns, reads=[HBs_b[j % 2], ident_b], writes=[PS_b[bank]])
            S.op("act", lambda e, j=j, pst=pst: e.activation(out=hT[:, :, j * 128:(j + 1) * 128], in_=pst.rearrange("p (k t) -> p k t", k=NKT), func=AF.Copy),
                 reads=[PS_b[bank]], writes=[hT_b])

        for i in range(16 + 3):
            if i < 16:
                e1_f0(i)
            if 0 <= i - 1 < 16:
                e1_f1(i - 1)
            if 0 <= i - 2 < 16:
                e1_f2(i - 2)
            if 0 <= i - 3 < 16:
                e1_back(i - 3)
        S.barrier()

        PF = NWS - 1
        kuse = 0
        for qt in range(opts.get('n_qt', 4)):
            tsl = slice(qt * 512, (qt + 1) * 512)
            for f in range(22):
                if f % 2 == 0:
                    while nloaded[0] < min(NLOAD, kuse + PF):
                        load_wf(nloaded[0])
                        nloaded[0] += 1
                sl = kuse % NWS
                fo = (f % 2) * 128
                bg, bu = (0, 1) if f % 2 == 0 else (2, 3)
                for which, bank in ((0, bg), (1, bu)):
                    fns = []
                    for kt in range(NKT):
                        fns.append(lambda e, kt=kt, sl=sl, which=which, bank=bank, tsl=tsl, fo=fo: e.matmul(
                            PS[bank], lhsT=WF[sl][:, kt, which, fo:fo + 128], rhs=hT[:, kt, tsl], start=(kt == 0), stop=(kt == NKT - 1)))
                    S.group("pe", fns, reads=[WF_b[sl], hT_b], writes=[PS_b[bank]])
                S.op("act", lambda e, bg=bg: e.activation(out=SG, in_=PS[bg], func=AF.Silu), reads=[PS_b[bg]], writes=[SG_b])
                S.op("dve", lambda e, bu=bu, f=f: e.tensor_tensor(out=AT[:, f, :], in0=PS[bu], in1=SG, op=ALU.mult),
                     reads=[PS_b[bu], SG_b], writes=[AT_b])
                if f % 2 == 1:
                    kuse += 1
            def load_h(jj):
                j = qt * 4 + jj
                xs = jj % 2
                S.dma("sp", lambda e, j=j, xs=xs: e.dma_start(out=XJ[xs], in_=hscr[j * 128:(j + 1) * 128, :]), XJ_b[xs],
                      reads=[hs_b[j]], writes=[XJ_b[xs]])
            load_h(0)
            load_h(1)
            for jj in range(4):
                j = qt * 4 + jj
                xs = jj % 2
                pre = PRE[jj % 2]
                pre_b = PRE_b[jj % 2]
                for nh in range(2):
                    bank = 4 + nh
                    fns = []
                    for f in range(22):
                        fns.append(lambda e, f=f, jj=jj, nh=nh, bank=bank: e.matmul(
                            PS[bank], lhsT=AT[:, f, jj * 128:(jj + 1) * 128], rhs=Wdn[:, f, nh * 512:(nh + 1) * 512],
                            start=(f == 0), stop=(f == 21)))
                    S.group("pe", fns, reads=[AT_b, Wdn_b], writes=[PS_b[bank]])
                    S.op("dve", lambda e, nh=nh, bank=bank, xs=xs, pre=pre: e.scalar_tensor_tensor(
                        out=pre[:, nh * 512:(nh + 1) * 512], in0=XJ[xs][:, nh * 512:(nh + 1) * 512], scalar=ALPHA,
                        in1=PS[bank], op0=ALU.mult, op1=ALU.add), reads=[XJ_b[xs], PS_b[bank]], writes=[pre_b])
                if jj + 2 < 4:
                    load_h(jj + 2)
                layer_norm(pre, pre, 2, pre_b, pre_b, jj % 2)
                out_toks.append(S.dma("sp", lambda e, j=j, pre=pre: e.dma_start(out=o_rows[j], in_=pre), pre_b, reads=[pre_b]))

    if upto in "D":
        MX, MX_b, Wo, Wo_b = stage_d()
        if not opts.get("skip_e", False):
            S.barrier()
            stage_e(MX, MX_b, Wo, Wo_b)

    dump_b = S.buf("dump")
    if dbg:
        S.barrier()
        for name in dbg:
            off, shape, dt = opts["dbg_src"][name]
            if off == "xT":
                t = xT[:]
            else:
                t = carve(off, shape, dt)
            out_toks.append(S.dma("sp", lambda e, name=name, t=t: e.dma_start(out=dbg_d[name], in_=t), dump_b))
    S.wait_only("sp", out_toks)
    S.build()
    es.close()
    return nc


_CACHE = {}


def _shared_inputs(inputs):
    f32 = np.float32
    c = _consts()
    a_re = np.asarray(inputs["ssm_a_re"], f32)[0]
    a_im = np.asarray(inputs["ssm_a_im"], f32)[0]
    ldt = np.asarray(inputs["ssm_log_dt"], f32)[0]
    b_re = np.asarray(inputs["ssm_b_re"], f32)[0]
    b_im = np.asarray(inputs["ssm_b_im"], f32)[0]
    c_re = np.asarray(inputs["ssm_c_re"], f32)[0]
    c_im = np.asarray(inputs["ssm_c_im"], f32)[0]
    dsk = np.asarray(inputs["ssm_d"], f32)[0]
    ar2 = np.tile(a_re.T, (2, 1))
    ai2 = np.tile(a_im.T, (2, 1))
    ldt2 = np.tile(ldt[None, :], (128, 1))
    dcol = np.tile(dsk.reshape(32, 16).T, (8, 1))
    bre = b_re.transpose(1, 0, 2).reshape(64, 512)
    bim = b_im.transpose(1, 0, 2).reshape(64, 512)
    cre = c_re.transpose(2, 0, 1).reshape(64, 512)
    cim = c_im.transpose(2, 0, 1).reshape(64, 512)
    BA = np.concatenate([bre, bim], 0)
    BB = np.concatenate([bim, bre], 0)
    CA = np.concatenate([cre, cim], 0)
    CB = np.concatenate([cim, cre], 0)
    ssmp = np.ascontiguousarray(np.concatenate([ar2, ai2, ldt2, dcol, BA, BB, CA, CB], axis=1).astype(f32))
    bg = np.asarray(inputs["b_gate"], f32)[0]
    bgate = np.ascontiguousarray(bg.reshape(2, 8, 128).transpose(2, 0, 1).reshape(128, 16))
    lnt = np.concatenate([np.asarray(inputs[k], f32)[0] for k in ("ln1_g", "ln1_b", "ln2_g", "ln2_b")])[None, :]
    lnt = np.ascontiguousarray(np.tile(lnt, (128, 1)))
    sh = {"ident": c["ident"], "cosT": c["cosT"], "sinT": c["sinT"], "ssmc": c["ssmc"], "ssmp": ssmp,
          "bgate": bgate, "lnt": lnt}
    for k in ("w_in", "w_glu", "w_attn_br", "w_ssm_br", "w_out", "w_ff_gate", "w_ff_up", "w_ff_down"):
        sh[k] = np.ascontiguousarray(np.asarray(inputs[k], f32)[0])
    return sh


def kernel(**inputs):
    x = np.asarray(inputs["x"], np.float32)
    nb = x.shape[0]
    if "nc" not in _CACHE:
        _CACHE["nc"] = build_program()
    nc = _CACHE["nc"]
    sh = _shared_inputs(inputs)
    in_maps = []
    for b in range(nb):
        m = dict(sh)
        m["x"] = np.ascontiguousarray(x[b])
        in_maps.append(m)
    res = run_bass_kernel_spmd(nc, in_maps, core_ids=list(range(nb)))
    out = np.stack([np.asarray(r["out"], np.float32) for r in res.results], axis=0)
    return out
```

```python
import math
from contextlib import ExitStack

import numpy as np
import ml_dtypes

import concourse.bass as bass
import concourse.mybir as mybir
from concourse.bass_utils import run_bass_kernel_spmd

F32 = mybir.dt.float32
BF16 = mybir.dt.bfloat16
I32 = mybir.dt.int32
AF = mybir.ActivationFunctionType
ALU = mybir.AluOpType

ENGS = ("pe", "act", "dve", "pool", "sp")

S_LEN = 2048
D = 1024
NKT = 8
DFF = 2816
ALPHA = 2.0 ** 0.25
LN_EPS = 1e-5
DIL = (1, 4, 16)


class Buf:
    __slots__ = ("w", "r", "name", "dsem", "excl")

    def __init__(self, name="", excl=False):
        self.w = {}
        self.r = {}
        self.name = name
        self.dsem = None
        self.excl = excl


class Sched:
    def __init__(self, nc):
        self.nc = nc
        self.streams = {e: [] for e in ENGS}
        self.cnt = {}
        self.sem_objs = {}
        self.sem_names = []
        self.waited = {e: {} for e in ENGS}
        self.nbuf = 0
        for e in ENGS:
            self.new_sem("c_" + e)

    def new_sem(self, name):
        assert name not in self.cnt
        self.cnt[name] = 0
        self.sem_names.append(name)
        return name

    def buf(self, name="", excl=False):
        self.nbuf += 1
        return Buf("%s_%d" % (name, self.nbuf), excl)

    def _deps(self, reads, writes, extra):
        deps = {}

        def add(d):
            for s, v in d.items():
                if deps.get(s, 0) < v:
                    deps[s] = v
        for b in reads:
            add(b.w)
            if b.excl:
                add(b.r)
        for b in writes:
            add(b.w)
            add(b.r)
        for t in extra:
            if t is not None:
                add({t[0]: t[1]})
        return deps

    def _waits(self, eng, deps):
        out = []
        for s, v in deps.items():
            if self.waited[eng].get(s, 0) >= v:
                continue
            self.waited[eng][s] = v
            out.append((s, v))
        return out

    def _mark(self, tok, reads, writes):
        s, v = tok
        for b in reads:
            if b.r.get(s, 0) < v:
                b.r[s] = v
        for b in writes:
            if b.w.get(s, 0) < v:
                b.w[s] = v

    def op(self, eng, fn, reads=(), writes=(), extra=()):
        w = self._waits(eng, self._deps(reads, writes, extra))
        s = "c_" + eng
        self.cnt[s] += 1
        tok = (s, self.cnt[s])
        self.streams[eng].append((w, [fn], tok, 1))
        self._mark(tok, reads, writes)
        return tok

    def group(self, eng, fns, reads=(), writes=(), extra=()):
        w = self._waits(eng, self._deps(reads, writes, extra))
        s = "c_" + eng
        self.cnt[s] += 1
        tok = (s, self.cnt[s])
        self.streams[eng].append((w, list(fns), tok, 1))
        self._mark(tok, reads, writes)
        return tok

    def dma(self, eng, fn, sembuf, reads=(), writes=(), extra=()):
        if sembuf.dsem is None:
            sembuf.dsem = self.new_sem("d_" + sembuf.name)
        deps = self._deps(reads, [], extra)
        for b in writes:
            for d_ in (b.r, b.w):
                for s_, v_ in d_.items():
                    if d_ is b.w and s_ == sembuf.dsem:
                        continue
                    if deps.get(s_, 0) < v_:
                        deps[s_] = v_
        w = self._waits(eng, deps)
        self.cnt[sembuf.dsem] += 16
        tok = (sembuf.dsem, self.cnt[sembuf.dsem])
        self.streams[eng].append((w, [fn], tok, 16))
        self._mark(tok, reads, writes)
        return tok

    def wait_only(self, eng, toks):
        deps = {}
        for t in toks:
            if t is not None and deps.get(t[0], 0) < t[1]:
                deps[t[0]] = t[1]
        w = self._waits(eng, deps)
        if w:
            self.streams[eng].append((w, [], None, 0))

    def barrier(self):
        toks = [(s, v) for s, v in self.cnt.items() if v > 0]
        for e in ENGS:
            self.wait_only(e, toks)

    def build(self):
        nc = self.nc
        with ExitStack() as es:
            for n in self.sem_names:
                self.sem_objs[n] = es.enter_context(nc.semaphore(n))
            block = es.enter_context(nc.Block())
            emap = {"pe": block.tensor, "act": block.scalar, "dve": block.vector,
                    "pool": block.gpsimd, "sp": block.sync}
            for e in ENGS:
                stream = self.streams[e]

                def body(eng, stream=stream):
                    for (w, fns, tok, inc) in stream:
                        for (s, v) in w:
                            eng.wait_ge(self.sem_objs[s], v)
                        ins = None
                        for fn in fns:
                            ins = fn(eng)
                        if tok is not None and ins is not None:
                            ins.then_inc(self.sem_objs[tok[0]], inc)
                emap[e](body)


TWO_PI = 2.0 * math.pi


def _consts():
    c = {}
    c["ident"] = np.eye(128, dtype=np.float32).astype(ml_dtypes.bfloat16)
    ida = np.eye(128, dtype=np.float32)
    idb = np.zeros((128, 128), np.float32)
    for p in range(64):
        idb[p, 64 + p] = 1.0
        idb[64 + p, p] = -1.0
    half = 32
    inv_freq = (np.float32(10000.0) ** (-np.arange(half, dtype=np.float32) / np.float32(half))).astype(np.float32)
    pos = np.arange(S_LEN, dtype=np.float32)
    ang = (pos[None, :] * inv_freq[:, None]).astype(np.float32)
    c["cosT"] = np.tile(np.cos(ang).astype(np.float32), (4, 1))
    c["sinT"] = np.tile(np.sin(ang).astype(np.float32), (4, 1))
    cm = np.zeros((128, 2, 256), np.float32)
    di = np.zeros((128, 2, 256), np.float32)
    for mh in range(2):
        for sl in range(8):
            s_ = mh * 8 + sl
            for hi in range(16):
                p = sl * 16 + hi
                for t in range(16):
                    if t >= s_:
                        cm[p, mh, t * 16:(t + 1) * 16] = 1.0
                di[p, mh, s_ * 16 + hi] = 1.0
    kn = np.tile(-np.arange(16, dtype=np.float32)[None, :], (128, 1))
    kp = np.tile(np.arange(32, dtype=np.float32)[None, :], (128, 1))
    cidx = np.tile(np.arange(128, dtype=np.float32)[None, :], (128, 1))
    sgn = np.ones((128, 1), np.float32)
    sgn[64:] = -1.0
    c["ssmc"] = np.concatenate([ida, idb, cm.reshape(128, 512), di.reshape(128, 512), kn, kp, cidx, sgn],
                               axis=1).astype(np.float32)
    return c


SSMC_W = 128 + 128 + 512 + 512 + 16 + 32 + 128 + 1


def _prod(xs):
    r = 1
    for v in xs:
        r *= int(v)
    return r


AW = 43464


def build_program(dbg=None, opts=None):
    opts = opts or {}
    upto = opts.get("upto", "D")
    nc = bass.Bass("TRN2", target_bir_lowering=False)

    def din(name, shape, dt=F32):
        return nc.dram_tensor(name, list(shape), dt, kind="ExternalInput").ap()

    x_d = din("x", [S_LEN, D])
    w_in_d = din("w_in", [D, 7168])
    ident_d = din("ident", [128, 128], BF16)
    cos_d = din("cosT", [128, S_LEN])
    sin_d = din("sinT", [128, S_LEN])
    ssmc_d = din("ssmc", [128, SSMC_W])
    ssmp_d = din("ssmp", [128, 4 * 32 + 4 * 512])
    bgate_d = din("bgate", [128, 16])
    lnt_d = din("lnt", [128, 4 * D])
    w_glu_d = din("w_glu", [512, 1024])
    w_ab_d = din("w_attn_br", [512, 1024])
    w_sb_d = din("w_ssm_br", [512, 1024])
    w_out_d = din("w_out", [D, D])
    w_fg_d = din("w_ff_gate", [D, DFF])
    w_fu_d = din("w_ff_up", [D, DFF])
    w_fd_d = din("w_ff_down", [DFF, D])
    out_d = nc.dram_tensor("out", [S_LEN, D], F32, kind="ExternalOutput").ap()
    dbg_d = {}
    if dbg:
        for name, (shape, dt) in dbg.items():
            dbg_d[name] = nc.dram_tensor("dbg_" + name, list(shape), dt, kind="ExternalOutput").ap()

    S = Sched(nc)
    es = ExitStack()

    def sb(name, shape, dt):
        return es.enter_context(nc.sbuf_tensor("s_" + name, list(shape), dt))

    ARENA = sb("arena", [128, AW], F32)

    def carve(off, shape, dt):
        n = _prod(shape[1:])
        if dt == BF16:
            assert n % 2 == 0
            ap = ARENA[:, off:off + n // 2].bitcast(BF16)
            words = n // 2
        elif dt == I32:
            ap = ARENA[:, off:off + n].bitcast(I32)
            words = n
        else:
            ap = ARENA[:, off:off + n]
            words = n
        assert off + words <= AW, (off, words)
        if len(shape) > 2:
            names = ["d%d" % i for i in range(len(shape) - 1)]
            pat = "p (%s) -> p %s" % (" ".join(names), " ".join(names))
            ap = ap.rearrange(pat, **{nm: int(v) for nm, v in zip(names[:-1], shape[1:-1])})
        return ap

    xT = sb("xT", [128, NKT, S_LEN], BF16)
    xT_b = S.buf("xT")
    ident = sb("ident", [128, 128], BF16)
    ident_b = S.buf("ident")
    ssmc = sb("ssmc", [128, SSMC_W], F32)
    ssmc_b = S.buf("ssmc")
    bgate = sb("bgate", [128, 16], F32)
    IDA = ssmc[:, 0:128]
    IDB = ssmc[:, 128:256]
    CM = ssmc[:, 256:768].rearrange("p (m c) -> p m c", m=2)
    DI = ssmc[:, 768:1280].rearrange("p (m c) -> p m c", m=2)
    KN = ssmc[:, 1280:1296]
    KP = ssmc[:, 1296:1328]
    CIDX = ssmc[:, 1328:1456]
    SGN = ssmc[:, 1456:1457]

    PSALL = es.enter_context(nc.psum_tensor("p_all", [128, 8, 512], F32))
    PS = [PSALL[:, i, :] for i in range(8)]
    PS_b = [S.buf("ps%d" % i, excl=True) for i in range(8)]

    out_toks = []
    S.dma("sp", lambda e: e.dma_start(out=ident[:], in_=ident_d), ident_b, writes=[ident_b])
    S.dma("sp", lambda e: e.dma_start(out=ssmc[:], in_=ssmc_d), ssmc_b, writes=[ssmc_b])
    S.dma("sp", lambda e: e.dma_start(out=bgate[:], in_=bgate_d), ssmc_b, writes=[ssmc_b])

    attnT = carve(0, [128, 4, S_LEN], BF16)
    attnT_b = S.buf("attnT")
    YgT = carve(4096, [128, 4, S_LEN], BF16)
    YgT_b = S.buf("YgT")

    NXS = 8
    xin = [carve(36000 + 512 * i, [128, D], BF16) for i in range(NXS)]
    xin_b = [S.buf("xin%d" % i) for i in range(NXS)]
    x_tiles = x_d.rearrange("(j p) d -> j p d", p=128)
    for j in range(16):
        sl = j % NXS
        S.dma("pool", lambda e, j=j, sl=sl: e.dma_start(out=xin[sl], in_=x_tiles[j]),
              xin_b[sl], writes=[xin_b[sl]])
        bank = 6 + (j % 2)
        pst = PS[bank].bitcast(BF16)
        fns = []
        for kt in range(NKT):
            fns.append(lambda e, kt=kt, sl=sl, pst=pst: e.transpose(
                pst[:, kt * 128:(kt + 1) * 128], xin[sl][:, kt * 128:(kt + 1) * 128], ident[:]))
        S.group("pe", fns, reads=[xin_b[sl], ident_b], writes=[PS_b[bank]])
        src = pst.rearrange("p (k t) -> p k t", k=NKT)
        dst = xT[:, :, j * 128:(j + 1) * 128]
        if j % 2 == 0:
            S.op("act", lambda e, src=src, dst=dst: e.activation(out=dst, in_=src, func=AF.Copy),
                 reads=[PS_b[bank]], writes=[xT_b])
        else:
            S.op("dve", lambda e, src=src, dst=dst: e.tensor_copy(out=dst, in_=src),
                 reads=[PS_b[bank]], writes=[xT_b])

    w_in_k = w_in_d.rearrange("(kt p) c -> p kt c", p=128)

    pref_c = {"Wu_b": S.buf("Wu"), "prm_b": S.buf("prm"), "done": False}

    def prefetch_c(extra_bufs=()):
        if pref_c["done"]:
            return
        pref_c["done"] = True
        Wu_ = carve(16384, [128, NKT, 512], BF16)
        PRM_ = carve(18432, [128, 4 * 32 + 4 * 512], F32)
        S.dma("pool", lambda e: e.dma_start(out=Wu_, in_=w_in_k[:, :, 4608:5120]), pref_c["Wu_b"],
              writes=[pref_c["Wu_b"]] + list(extra_bufs))
        S.dma("sp", lambda e: e.dma_start(out=PRM_, in_=ssmp_d), pref_c["prm_b"],
              writes=[pref_c["prm_b"]] + list(extra_bufs))

    def stage_b():
        cosT = carve(4096, [128, S_LEN], F32)
        sinT = carve(6144, [128, S_LEN], F32)
        tab_b = S.buf("ropetab")
        S.dma("sp", lambda e: e.dma_start(out=cosT, in_=cos_d), tab_b, writes=[tab_b])
        S.dma("sp", lambda e: e.dma_start(out=sinT, in_=sin_d), tab_b, writes=[tab_b])
        ACC = carve(8192, [128, 4, S_LEN], F32)
        ACC_b = S.buf("ACC")
        RT = [[carve(16384 + (i * 4 + k) * 512, [128, 512], F32) for k in range(4)] for i in range(2)]
        RT_b = [[S.buf("rt") for k in range(4)] for i in range(2)]
        WB = [carve(20480 + 3072 * i, [128, NKT, 768], BF16) for i in range(2)]
        WB_b = [S.buf("wb%d" % i) for i in range(2)]
        Qg = carve(26624, [128, 2, S_LEN], BF16)
        Kg = carve(28672, [128, 2, S_LEN], BF16)
        Q_b, K_b = S.buf("Qg"), S.buf("Kg")
        Vaug = carve(30720, [128, 16, 4, 128], BF16)
        V_b = S.buf("Vaug")
        PB = [carve(34816 + 512 * i, [128, 4, 2, 128], BF16) for i in range(2)]
        PB_b = [S.buf("pb") for i in range(2)]

        S.op("pool", lambda e: e.memset(Vaug[:, :, :, 64:128], 1.0), writes=[V_b])

        WST = carve(40200, [128, 2, NKT, 256], BF16)
        WST_b = S.buf("wst")

        def load_weights(it):
            hq, g = divmod(it, 3)
            sl = it % 2
            wb = WB[sl]
            for qk in range(2):
                base = qk * 1536 + g * 512 + hq * 256
                S.dma("pool", lambda e, base=base, qk=qk: e.dma_start(out=WST[:, qk, :, :], in_=w_in_k[:, :, base:base + 256]),
                      WST_b, writes=[WST_b])
            base = 3072 + g * 512 + hq * 256
            S.dma("pool", lambda e, base=base, wb=wb: e.dma_start(out=wb[:, :, 512:768], in_=w_in_k[:, :, base:base + 256]),
                  WB_b[sl], writes=[WB_b[sl]])
            for qk in range(2):
                src = WST[:, qk, :, :].rearrange("p k (h t i) -> p k t h i", h=4, t=2)
                dst = wb[:, :, qk * 256:(qk + 1) * 256].rearrange("p k (t h i) -> p k t h i", t=2, h=4)
                for half in range(2):
                    S.op("act", lambda e, src=src, dst=dst, half=half: e.activation(
                        out=dst[:, :, half, :, :], in_=src[:, :, half, :, :], func=AF.Copy),
                        reads=[WST_b], writes=[WB_b[sl]])

        ZT0 = carve(40200, [128, S_LEN], F32)
        ZT1 = carve(36000, [128, 4, S_LEN], BF16)
        ZT_b = [S.buf("zt") for i in range(2)]
        pending = [None]

        def emit_norm_act(hq_):
            for hh in range(4):
                S.op("act", lambda e, hh=hh: e.activation(out=ZT0[64:128, :], in_=ACC[64:128, hh, :], func=AF.Ln),
                     reads=[ACC_b], writes=[ZT_b[0], WST_b])
                S.op("act", lambda e, hh=hh: e.activation(out=ZT1[0:64, hh, :], in_=ZT0[64:128, :], func=AF.Exp, scale=-1.0),
                     reads=[ZT_b[0], WST_b], writes=[ZT_b[1]])

        def emit_norm(hq_):
            for hh in range(4):
                h_glob = hq_ * 4 + hh
                pbase = 64 * (h_glob % 2)
                dst = attnT[pbase:pbase + 64, h_glob // 2, :]
                S.op("dve", lambda e, hh=hh, dst=dst: e.tensor_tensor(out=dst, in0=ACC[0:64, hh, :], in1=ZT1[0:64, hh, :], op=ALU.mult),
                     reads=[ACC_b, ZT_b[1]], writes=[attnT_b])

        load_weights(0)
        chunk_ctr = [0]
        for it in range(6):
            hq, g = divmod(it, 3)
            d = DIL[g]
            sl = it % 2
            wb = WB[sl]
            if it + 1 < 6:
                load_weights(it + 1)
            for qk in range(2):
                dstT, dst_b = (Qg, Q_b) if qk == 0 else (Kg, K_b)
                for tc in range(4):
                    cc = chunk_ctr[0]
                    chunk_ctr[0] += 1
                    b1, b2 = (0, 1) if cc % 2 == 0 else (2, 3)
                    for half, bank in ((0, b1), (1, b2)):
                        off = (qk * 2 + half) * 128
                        fns = []
                        for kt in range(NKT):
                            fns.append(lambda e, kt=kt, off=off, bank=bank, tc=tc, wb=wb: e.matmul(
                                PS[bank], lhsT=wb[:, kt, off:off + 128], rhs=xT[:, kt, tc * 512:(tc + 1) * 512],
                                start=(kt == 0), stop=(kt == NKT - 1)))
                        S.group("pe", fns, reads=[WB_b[sl], xT_b], writes=[PS_b[bank]])
                    rs = cc % 2
                    ra, rb, rc, rd = RT[rs]
                    ra_b, rb_b, rc_b, rd_b = RT_b[rs]
                    cs = cosT[:, tc * 512:(tc + 1) * 512]
                    sn = sinT[:, tc * 512:(tc + 1) * 512]
                    dd = d if qk == 0 else 1
                    npc = 512 // dd

                    def pv_(ap, dd=dd):
                        return ap.rearrange("p (m r) -> p r m", r=dd)

                    def cv_(ap, dd=dd):
                        return ap.rearrange("p (r m) -> p r m", r=dd)
                    S.op("dve", lambda e, ra=ra, b1=b1, cs=cs, pv_=pv_, cv_=cv_: e.tensor_tensor(out=cv_(ra), in0=pv_(PS[b1]), in1=pv_(cs), op=ALU.mult),
                         reads=[PS_b[b1], tab_b], writes=[ra_b])
                    S.op("dve", lambda e, rc=rc, b1=b1, sn=sn, pv_=pv_, cv_=cv_: e.tensor_tensor(out=cv_(rc), in0=pv_(PS[b1]), in1=pv_(sn), op=ALU.mult),
                         reads=[PS_b[b1], tab_b], writes=[rc_b])
                    S.op("dve", lambda e, rb=rb, b2=b2, sn=sn, pv_=pv_, cv_=cv_: e.tensor_tensor(out=cv_(rb), in0=pv_(PS[b2]), in1=pv_(sn), op=ALU.mult),
                         reads=[PS_b[b2], tab_b], writes=[rb_b])
                    S.op("dve", lambda e, rd=rd, b2=b2, cs=cs, pv_=pv_, cv_=cv_: e.tensor_tensor(out=cv_(rd), in0=pv_(PS[b2]), in1=pv_(cs), op=ALU.mult),
                         reads=[PS_b[b2], tab_b], writes=[rd_b])
                    for half, (i0, i1, op) in enumerate(((ra, rb, ALU.subtract), (rc, rd, ALU.add))):
                        i0b, i1b = ((ra_b, rb_b), (rc_b, rd_b))[half]
                        dview = dstT[:, half, :].rearrange("p (r m) -> p r m", r=dd)[:, :, npc * tc:npc * (tc + 1)]
                        S.op("dve" if half == 0 else "pool", lambda e, dview=dview, i0=i0, i1=i1, op=op, cv_=cv_: e.tensor_tensor(
                            out=dview, in0=cv_(i0), in1=cv_(i1), op=op), reads=[i0b, i1b], writes=[dst_b])
            nb = S_LEN // (128 * d)
            for bp in range(8):
                cc = chunk_ctr[0]
                chunk_ctr[0] += 1
                bank = cc % 4
                for sub in range(2):
                    blk = bp * 2 + sub
                    r, b = divmod(blk, nb)
                    fns = []
                    for kt in range(NKT):
                        lt = xT[:, kt, b * 128 * d:(b + 1) * 128 * d].rearrange("p (a r) -> p a r", r=d)[:, :, r]
                        fns.append(lambda e, kt=kt, lt=lt, bank=bank, sub=sub, wb=wb: e.matmul(
                            PS[bank][:, sub * 256:(sub + 1) * 256], lhsT=lt, rhs=wb[:, kt, 512:768],
                            start=(kt == 0), stop=(kt == NKT - 1)))
                    S.group("pe", fns, reads=[WB_b[sl], xT_b], writes=[PS_b[bank]])
                src = PS[bank].rearrange("p (s h c) -> p s h c", s=2, h=4)
                dst = Vaug[:, bp * 2:bp * 2 + 2, :, 0:64]
                S.op("act", lambda e, src=src, dst=dst: e.activation(out=dst, in_=src, func=AF.Copy),
                     reads=[PS_b[bank]], writes=[V_b])
            if pending[0] is not None:
                emit_norm(pending[0])
                pending[0] = None
            if it == 5:
                prefetch_c([b_ for row in RT_b for b_ in row] + [WB_b[0]])
            blocks = []
            for blk in range(16):
                r, b = divmod(blk, nb)
                kts = ([("prev", blk - 1)] if b > 0 else []) + [("cur", blk)]
                blocks.append((blk, r, b, kts))

            def kview(hh, half, kblk, d=d, nb=nb):
                kr, kb = divmod(kblk, nb)
                return Kg[32 * hh:32 * hh + 32, half, kb * 128 * d:(kb + 1) * 128 * d].rearrange("p (a r) -> p a r", r=d)[:, :, kr]

            def emit_qk(bi):
                blk, r, b, kts = blocks[bi]
                par = bi % 2
                nk = len(kts)
                fns = []
                for hh in range(4):
                    for ki, (kind, kblk) in enumerate(kts):
                        for half in range(2):
                            fns.append(lambda e, hh=hh, half=half, kblk=kblk, blk=blk, ki=ki, kview=kview: e.matmul(
                                PSALL[:, hh, ki * 128:(ki + 1) * 128],
                                lhsT=kview(hh, half, kblk),
                                rhs=Qg[32 * hh:32 * hh + 32, half, blk * 128:(blk + 1) * 128],
                                start=(half == 0), stop=(half == 1), tile_position=(32 * hh, 0)))
                S.group("pe", fns, reads=[Q_b, K_b], writes=PS_b[0:4])
                S.op("act", lambda e, par=par, nk=nk: e.activation(
                    out=PB[par][:, :, 0:nk, :], in_=PSALL[:, 0:4, 0:nk * 128].rearrange("p h (k q) -> p h k q", k=nk),
                    func=AF.Exp, scale=0.125), reads=PS_b[0:4], writes=[PB_b[par]])
                for ki, (kind, kblk) in enumerate(kts):
                    pv = PB[par][:, :, ki, :]
                    if kind == "cur":
                        S.op("pool", lambda e, pv=pv: e.affine_select(
                            out=pv, in_=pv, pattern=[[0, 4], [1, 128]], base=0, channel_multiplier=-1,
                            compare_op=ALU.is_ge, fill=0.0), writes=[PB_b[par]])
                    else:
                        S.op("pool", lambda e, pv=pv: e.affine_select(
                            out=pv, in_=pv, pattern=[[0, 4], [-1, 128]], base=0, channel_multiplier=1,
                            compare_op=ALU.is_ge, fill=0.0), writes=[PB_b[par]])

            def emit_pv(bi):
                blk, r, b, kts = blocks[bi]
                obank = 4 + (bi % 2)
                fns = []
                par = bi % 2
                nk = len(kts)
                for hh in range(4):
                    for ki, (kind, kblk) in enumerate(kts):
                        fns.append(lambda e, hh=hh, kblk=kblk, par=par, ki=ki, obank=obank, nk=nk: e.matmul(
                            PS[obank][:, hh * 128:(hh + 1) * 128], lhsT=Vaug[:, kblk, hh, :],
                            rhs=PB[par][:, hh, ki, :], start=(ki == 0), stop=(ki == nk - 1)))
                S.group("pe", fns, reads=[V_b, PB_b[par]], writes=[PS_b[obank]])
                if d == 1:
                    dst = ACC.rearrange("p h (r m) -> p h m r", r=16)[:, :, 8 * b:8 * b + 8, :]
                    src = PS[obank].rearrange("p (h m r) -> p h m r", h=4, r=16)
                elif d == 4:
                    dst = ACC.rearrange("p h (q s m) -> p h m q s", q=4, s=4)[:, :, 32 * b:32 * b + 32, :, r]
                    src = PS[obank].rearrange("p (h m q) -> p h m q", h=4, q=4)
                else:
                    dst = ACC[:, :, r * 128:(r + 1) * 128]
                    src = PS[obank].rearrange("p (h m) -> p h m", h=4)
                if g == 0:
                    S.op("dve", lambda e, dst=dst, src=src: e.tensor_copy(out=dst, in_=src),
                         reads=[PS_b[obank]], writes=[ACC_b])
                else:
                    S.op("dve", lambda e, dst=dst, src=src: e.tensor_tensor(out=dst, in0=src, in1=dst, op=ALU.add),
                         reads=[PS_b[obank]], writes=[ACC_b])

            emit_qk(0)
            for bi in range(16):
                if bi + 1 < 16:
                    emit_qk(bi + 1)
                emit_pv(bi)

            if g == 2:
                emit_norm_act(hq)
                pending[0] = hq
        if pending[0] is not None:
            emit_norm(pending[0])
            pending[0] = None

    if upto in "BCD" and not opts.get("skip_b", False):
        stage_b()
        S.barrier()

    def stage_c():
        base = [8192]

        def alloc(shape, dt):
            n = _prod(shape[1:])
            words = n // 2 if dt == BF16 else n
            ap = carve(base[0], shape, dt)
            base[0] += words
            return ap

        U_sb = alloc([128, 32, 16, 16], BF16)
        U_b = S.buf("U")
        X1 = alloc([128, 32, 16], F32)
        X2 = alloc([128, 32, 16], F32)
        Y1 = alloc([128, 32, 16], F32)
        Y2 = alloc([128, 32, 16], F32)
        LS = alloc([128, 8, 256], F32)
        assert base[0] == 16384
        Wu = alloc([128, NKT, 512], BF16)
        PRM = alloc([128, 4 * 32 + 4 * 512], F32)
        Wu_b, prm_b = pref_c["Wu_b"], pref_c["prm_b"]
        prefetch_c()
        AR, AI, LDT, DCOL = (PRM[:, 32 * i:32 * (i + 1)] for i in range(4))
        BA, BB, CA, CB = (PRM[:, 128 + 512 * i:128 + 512 * (i + 1)].rearrange("p (g h) -> p g h", g=32) for i in range(4))
        SC = alloc([128, 26, 32], F32)
        XR = alloc([128, 8, 256], F32)
        PNr = alloc([128, 32, 16], F32)
        PNi = alloc([128, 32, 16], F32)
        PPr = alloc([128, 32, 32], F32)
        PPi = alloc([128, 32, 32], F32)
        ta_off = base[0]
        TA = alloc([128, 8, 256], F32)
        XT1 = carve(ta_off, [128, 32, 16], F32)
        TB = alloc([128, 8, 256], F32)
        Wintra = alloc([128, 8, 2, 256], BF16)
        W1T = alloc([128, 8, 2, 128], BF16)
        WdS = alloc([128, 8, 256], BF16)
        UL = alloc([128, 8, 2, 128], BF16)
        BM1 = alloc([128, 1024], F32)
        BM2 = alloc([128, 1024], F32)
        BSW = alloc([128, 1024], F32)
        BGS = alloc([128, 1024], F32)
        EC = alloc([128, 1024], F32)
        ES2 = alloc([128, 1024], F32)
        HP = alloc([128, 8, 128], BF16)
        Ytok = alloc([128, 16, 128], BF16)
        KI = BSW.bitcast(I32)
        assert base[0] <= AW, base[0]
        g_b = S.buf("gen")
        q_b = S.buf("qgen")
        wts_b = S.buf("wts")
        UL_b = S.buf("UL")
        scan_b = S.buf("scan")
        tabs_b = S.buf("tabs")
        HP_b = S.buf("HP")
        Yt_b = S.buf("Ytok")

        def sc(i):
            return SC[:, i, :]

        def dve_tt(out, in0, in1, op, rd, wr):
            S.op("dve", lambda e: e.tensor_tensor(out=out, in0=in0, in1=in1, op=op), reads=rd, writes=wr)

        def dve_ts(out, in0, s1, op0, s2=None, op1=None, rd=(), wr=()):
            if op1 is None:
                S.op("dve", lambda e: e.tensor_scalar(out=out, in0=in0, scalar1=s1, scalar2=None, op0=op0), reads=rd, writes=wr)
            else:
                S.op("dve", lambda e: e.tensor_scalar(out=out, in0=in0, scalar1=s1, scalar2=s2, op0=op0, op1=op1), reads=rd, writes=wr)

        def act(out, in_, func, rd, wr, scale=1.0, bias=0.0):
            S.op("act", lambda e: e.activation(out=out, in_=in_, func=func, scale=scale, bias=bias), reads=rd, writes=wr)

        def sincos(x, n_shape_view, sin_out, cos_out, tmp_u, tmp_k, tmp_i, rd, wr):
            dve_ts(tmp_u, x, 1.0 / TWO_PI, ALU.mult, rd=rd, wr=wr)
            S.op("dve", lambda e: e.tensor_copy(out=tmp_i, in_=tmp_u), reads=rd, writes=wr)
            S.op("dve", lambda e: e.tensor_copy(out=tmp_k, in_=tmp_i), reads=rd, writes=wr)
            dve_tt(tmp_u, tmp_u, tmp_k, ALU.subtract, rd, wr)
            if sin_out is not None:
                act(sin_out, tmp_u, AF.Sin, rd, wr, scale=TWO_PI, bias=0.0)
            if cos_out is not None:
                dve_ts(tmp_k, tmp_u, -1.0, ALU.mult, rd=rd, wr=wr)
                dve_tt(tmp_k, tmp_k, tmp_u, ALU.max, rd, wr)
                act(cos_out, tmp_k, AF.Sin, rd, wr, scale=-TWO_PI, bias=math.pi / 2)

        G = [g_b, prm_b, ssmc_b]
        (iDT, iARDT, iTH, iMAG, iSIN, iCOS, iABR, iABI, iNRE, iT1, iT2, iDEN, iCR, iCI,
         iRHO16, iTH16, iPH16, iU, iK) = range(19)
        KIs = KI[:, 0:32]
        act(sc(iDT), LDT, AF.Exp, G, [g_b])
        dve_tt(sc(iARDT), AR, sc(iDT), ALU.mult, G, [g_b])
        dve_tt(sc(iTH), AI, sc(iDT), ALU.mult, G, [g_b])
        act(sc(iMAG), sc(iARDT), AF.Exp, G, [g_b])
        sincos(sc(iTH), None, sc(iSIN), sc(iCOS), sc(iU), sc(iK), KIs, G, [g_b])
        dve_tt(sc(iABR), sc(iMAG), sc(iCOS), ALU.mult, G, [g_b])
        dve_tt(sc(iABI), sc(iMAG), sc(iSIN), ALU.mult, G, [g_b])
        dve_ts(sc(iNRE), sc(iABR), -1.0, ALU.add, rd=G, wr=[g_b])
        dve_tt(sc(iT1), AR, AR, ALU.mult, G, [g_b])
        dve_tt(sc(iT2), AI, AI, ALU.mult, G, [g_b])
        dve_tt(sc(iDEN), sc(iT1), sc(iT2), ALU.add, G, [g_b])
        S.op("dve", lambda e: e.reciprocal(out=sc(iDEN), in_=sc(iDEN)), reads=G, writes=[g_b])
        dve_tt(sc(iT1), sc(iNRE), AR, ALU.mult, G, [g_b])
        dve_tt(sc(iT2), sc(iABI), AI, ALU.mult, G, [g_b])
        dve_tt(sc(iT1), sc(iT1), sc(iT2), ALU.add, G, [g_b])
        dve_tt(sc(iCR), sc(iT1), sc(iDEN), ALU.mult, G, [g_b])
        dve_tt(sc(iT1), sc(iABI), AR, ALU.mult, G, [g_b])
        dve_tt(sc(iT2), sc(iNRE), AI, ALU.mult, G, [g_b])
        dve_tt(sc(iT1), sc(iT1), sc(iT2), ALU.subtract, G, [g_b])
        dve_tt(sc(iCI), sc(iT1), sc(iDEN), ALU.mult, G, [g_b])
        act(sc(iRHO16), sc(iARDT), AF.Exp, G, [g_b], scale=16.0)
        dve_ts(sc(iTH16), sc(iTH), 16.0 / TWO_PI, ALU.mult, rd=G, wr=[g_b])
        S.op("dve", lambda e: e.tensor_copy(out=KIs, in_=sc(iTH16)), reads=G + [scan_b], writes=[g_b, scan_b])
        S.op("dve", lambda e: e.tensor_copy(out=sc(iK), in_=KIs), reads=G + [scan_b], writes=[g_b, scan_b])
        dve_tt(sc(iTH16), sc(iTH16), sc(iK), ALU.subtract, G, [g_b])
        dve_ts(sc(iPH16), sc(iTH16), TWO_PI, ALU.mult, rd=G, wr=[g_b])
        dve_ts(BB[0:64], BB[0:64], -1.0, ALU.mult, rd=G, wr=[g_b])
        CRb = SC[:, iCR, :].rearrange("p (g o) -> p g o", o=1).to_broadcast([128, 32, 16])
        CIb = SC[:, iCI, :].rearrange("p (g o) -> p g o", o=1).to_broadcast([128, 32, 16])
        dve_tt(X1, BA, CRb, ALU.mult, G, [g_b])
        dve_tt(XT1, BB, CIb, ALU.mult, G, [g_b])
        dve_tt(X1, X1, XT1, ALU.add, G, [g_b])
        dve_tt(X2, BB, CRb, ALU.mult, G, [g_b])
        dve_tt(XT1, BA, CIb, ALU.mult, G, [g_b])
        dve_tt(X2, X2, XT1, ALU.subtract, G, [g_b])
        dve_ts(Y1, CA, 1.0, ALU.mult, rd=G, wr=[g_b])
        dve_ts(Y1[64:128], CA[64:128], -1.0, ALU.mult, rd=G, wr=[g_b])
        dve_ts(Y2, CB, -1.0, ALU.mult, rd=G, wr=[g_b])

        ardt_a = SC[:, iARDT, :].rearrange("p (g o) -> p g o", o=1)
        th_a = SC[:, iTH, :].rearrange("p (g o) -> p g o", o=1)
        for (KT, n, Pr, Pi) in ((KN, 16, PNr, PNi), (KP, 32, PPr, PPi)):
            kb = KT.rearrange("p (o k) -> p o k", o=1).to_broadcast([128, 32, n])
            ta = TA.rearrange("p g c -> p (g c)")[:, 0:32 * n].rearrange("p (g k) -> p g k", g=32)
            tb = TB.rearrange("p g c -> p (g c)")[:, 0:32 * n].rearrange("p (g k) -> p g k", g=32)
            tu = BM1[:, 0:32 * n].rearrange("p (g k) -> p g k", g=32)
            tk = BM2[:, 0:32 * n].rearrange("p (g k) -> p g k", g=32)
            ti = KI[:, 0:32 * n].rearrange("p (g k) -> p g k", g=32)
            GG = G + [scan_b]
            dve_tt(ta, kb, ardt_a.to_broadcast([128, 32, n]), ALU.mult, GG, [g_b])
            act(ta, ta, AF.Exp, GG, [g_b])
            dve_tt(tb, kb, th_a.to_broadcast([128, 32, n]), ALU.mult, GG, [g_b])
            sincos(tb, None, Pi, Pr, tu, tk, ti, GG, [g_b, scan_b])
            dve_tt(Pr, Pr, ta, ALU.mult, GG, [g_b])
            dve_tt(Pi, Pi, ta, ALU.mult, GG, [g_b])

        for s_ in range(16):
            bank = s_ % 2
            fns = []
            for kt in range(NKT):
                lt = xT[:, kt, :].rearrange("p (c s) -> p c s", s=16)[:, :, s_]
                fns.append(lambda e, kt=kt, lt=lt, bank=bank: e.matmul(
                    PS[bank], lhsT=lt, rhs=Wu[:, kt, :], start=(kt == 0), stop=(kt == NKT - 1)))
            S.group("pe", fns, reads=[xT_b, Wu_b], writes=[PS_b[bank]])
            src = PS[bank].rearrange("p (g h) -> p g h", g=32)
            dst = U_sb[:, :, s_, :]
            S.op("act", lambda e, src=src, dst=dst: e.activation(out=dst, in_=src, func=AF.Copy),
                 reads=[PS_b[bank]], writes=[U_b])

        S.op("pool", lambda e: e.memset(HP[:, :, 0:1], 0.0), writes=[HP_b])

        ls_b, ta_b, tb_b = S.buf("ls"), S.buf("ta"), S.buf("tb")
        LS4 = LS.rearrange("p g (s h) -> p g s h", s=16)
        TA4 = TA.rearrange("p g (s h) -> p g s h", s=16)
        TB4 = TB.rearrange("p g (s h) -> p g s h", s=16)
        XR4 = XR.rearrange("p g (t h) -> p g t h", t=16)
        Gq = [g_b, q_b]

        def bc(ap, pat):
            return ap.rearrange(pat, o=1).to_broadcast([128, 8, 16, 16])

        def prod_dve(q_):
            gs_ = slice(8 * q_, 8 * q_ + 8)
            dve_tt(LS4, bc(PNr[:, gs_, :], "p g (s o) -> p g s o"), bc(X1[:, gs_, :], "p g (o h) -> p g o h"), ALU.mult, Gq, [ls_b])
            dve_tt(TA4, bc(PNi[:, gs_, :], "p g (s o) -> p g s o"), bc(X2[:, gs_, :], "p g (o h) -> p g o h"), ALU.mult, Gq, [ta_b])
            dve_tt(LS, LS, TA, ALU.add, [ta_b], [ls_b])
            dve_tt(XR4, bc(PPr[:, gs_, 0:16], "p g (t o) -> p g t o"), bc(Y1[:, gs_, :], "p g (o h) -> p g o h"), ALU.mult, Gq, [ls_b])
            dve_tt(TA4, bc(PPi[:, gs_, 0:16], "p g (t o) -> p g t o"), bc(Y2[:, gs_, :], "p g (o h) -> p g o h"), ALU.mult, Gq, [ta_b])
            dve_tt(XR, XR, TA, ALU.add, [ta_b], [ls_b])

        def prod_pool(q_):
            gs_ = slice(8 * q_, 8 * q_ + 8)
            a0_, a1_ = bc(PPr[:, gs_, 16:32], "p g (t o) -> p g t o"), bc(Y1[:, gs_, :], "p g (o h) -> p g o h")
            b0_, b1_ = bc(PPi[:, gs_, 16:32], "p g (t o) -> p g t o"), bc(Y2[:, gs_, :], "p g (o h) -> p g o h")
            S.op("pool", lambda e: e.tensor_tensor(out=TB4, in0=a0_, in1=a1_, op=ALU.mult), reads=Gq, writes=[tb_b])
            S.op("pool", lambda e: e.tensor_tensor(out=TA4, in0=b0_, in1=b1_, op=ALU.mult), reads=Gq, writes=[ta_b])
            S.op("pool", lambda e: e.tensor_tensor(out=WdS.rearrange("p g c -> p (g c)"), in0=TB.rearrange("p g c -> p (g c)"),
                                                   in1=TA.rearrange("p g c -> p (g c)"), op=ALU.add),
                 reads=[tb_b, ta_b], writes=[wts_b])

        def tables(q_):
            gs_ = slice(8 * q_, 8 * q_ + 8)
            T_ = [tabs_b, g_b, ssmc_b]
            ph_q = SC[:, iPH16, gs_].rearrange("p (g o) -> p g o", o=1).to_broadcast([128, 8, 128])
            cidb = CIDX.rearrange("p (o c) -> p o c", o=1).to_broadcast([128, 8, 128])
            TAf = TA.rearrange("p g c -> p (g c)")[:, 0:1024]
            TAi = TAf.bitcast(I32)
            EC3 = EC.rearrange("p (g c) -> p g c", g=8)
            TQ = T_ + [ta_b]
            dve_tt(EC3, cidb, ph_q, ALU.mult, T_, [tabs_b])
            dve_ts(EC, EC, 1.0 / TWO_PI, ALU.mult, rd=T_, wr=[tabs_b])
            S.op("dve", lambda e, TAi=TAi: e.tensor_copy(out=TAi, in_=EC), reads=TQ, writes=[tabs_b, ta_b])
            S.op("dve", lambda e, TAi=TAi: e.tensor_copy(out=ES2, in_=TAi), reads=TQ, writes=[tabs_b, ta_b])
            dve_tt(EC, EC, ES2, ALU.subtract, T_, [tabs_b])
            act(ES2, EC, AF.Sin, T_, [tabs_b], scale=TWO_PI, bias=0.0)
            dve_ts(TAf, EC, -1.0, ALU.mult, rd=TQ, wr=[tabs_b, ta_b])
            dve_tt(TAf, TAf, EC, ALU.max, TQ, [tabs_b, ta_b])
            S.op("act", lambda e, TAf=TAf: e.activation(out=EC, in_=TAf, func=AF.Sin, scale=-TWO_PI, bias=math.pi / 2),
                 reads=TQ, writes=[tabs_b, ta_b])
            dve_ts(ES2, ES2, SGN, ALU.mult, rd=T_, wr=[tabs_b])

        prod_dve(0)
        prod_pool(0)
        for q in range(4):
            gs = slice(8 * q, 8 * q + 8)
            Q = [ls_b, ssmc_b]
            for g8 in range(8):
                gg = 8 * q + g8
                bank = 2 + (g8 % 2)
                fns = []
                for mh in range(2):
                    fns.append(lambda e, g8=g8, mh=mh, bank=bank: e.matmul(
                        PS[bank][:, mh * 256:(mh + 1) * 256], lhsT=LS[:, g8, mh * 128:(mh + 1) * 128],
                        rhs=XR[:, g8, :], start=True, stop=True))
                S.group("pe", fns, reads=Q, writes=[PS_b[bank]])
                S.op("dve", lambda e, g8=g8, bank=bank: e.tensor_tensor(
                    out=BSW[:, 0:512], in0=PS[bank], in1=CM.rearrange("p m c -> p (m c)"), op=ALU.mult),
                    reads=[PS_b[bank], ssmc_b], writes=[scan_b])
                S.op("dve", lambda e, g8=g8, gg=gg: e.scalar_tensor_tensor(
                    out=Wintra[:, g8, :, :].rearrange("p m c -> p (m c)"), in0=DI.rearrange("p m c -> p (m c)"),
                    scalar=DCOL[:, gg:gg + 1], in1=BSW[:, 0:512], op0=ALU.mult, op1=ALU.add),
                    reads=[scan_b, ssmc_b, prm_b], writes=[wts_b])
                bank2 = 4 + (g8 % 2)
                fns = []
                for mh in range(2):
                    fns.append(lambda e, g8=g8, mh=mh, bank2=bank2: e.matmul(
                        PS[bank2][:, mh * 128:(mh + 1) * 128], lhsT=LS[:, g8, mh * 128:(mh + 1) * 128],
                        rhs=IDA, start=True, stop=True))
                S.group("pe", fns, reads=Q, writes=[PS_b[bank2]])
                S.op("act", lambda e, g8=g8, bank2=bank2: e.activation(
                    out=W1T[:, g8, :, :], in_=PS[bank2][:, 0:256].rearrange("p (m c) -> p m c", m=2), func=AF.Copy),
                    reads=[PS_b[bank2]], writes=[wts_b])
            for hb in range(2):
                bank = 6 + hb
                pst = PS[bank].bitcast(BF16)
                fns = []
                for k in range(8):
                    g8, mh = divmod(hb * 8 + k, 2)
                    gg = 8 * q + g8
                    src = U_sb[:, gg, mh * 8:(mh + 1) * 8, :].rearrange("p s h -> p (s h)")
                    fns.append(lambda e, k=k, src=src, pst=pst: e.transpose(pst[:, k * 128:(k + 1) * 128], src, ident[:]))
                S.group("pe", fns, reads=[U_b, ident_b], writes=[PS_b[bank]])
                dst = UL[:, hb * 4:(hb + 1) * 4, :, :]
                srcp = pst.rearrange("p (g m c) -> p g m c", g=4, m=2)
                S.op("act", lambda e, dst=dst, srcp=srcp: e.activation(out=dst, in_=srcp, func=AF.Copy),
                     reads=[PS_b[bank]], writes=[UL_b])
            if q + 1 < 4:
                prod_dve(q + 1)
            if q == 0:
                tables(0)
            for hb in range(2):
                fns = []
                for k in range(4):
                    g8 = hb * 4 + k
                    for mh in range(2):
                        fns.append(lambda e, g8=g8, mh=mh, k=k, hb=hb: e.matmul(
                            PS[hb][:, k * 128:(k + 1) * 128], lhsT=W1T[:, g8, mh, :], rhs=UL[:, g8, mh, :],
                            start=(mh == 0), stop=(mh == 1)))
                S.group("pe", fns, reads=[wts_b, UL_b], writes=[PS_b[hb]])
                cs_ = slice(hb * 512, (hb + 1) * 512)
                S.op("dve", lambda e, hb=hb, cs_=cs_: e.tensor_tensor(out=BM1[:, cs_], in0=PS[hb], in1=EC[:, cs_], op=ALU.mult),
                     reads=[PS_b[hb], tabs_b], writes=[scan_b])
                S.op("act", lambda e, hb=hb, cs_=cs_: e.activation(out=BSW[0:64, cs_], in_=PS[hb][64:128, :], func=AF.Copy),
                     reads=[PS_b[hb]], writes=[scan_b])
                S.op("act", lambda e, hb=hb, cs_=cs_: e.activation(out=BSW[64:128, cs_], in_=PS[hb][0:64, :], func=AF.Copy),
                     reads=[PS_b[hb]], writes=[scan_b])
            R_ = [scan_b, tabs_b]
            dve_tt(BM2, BSW, ES2, ALU.mult, R_, [scan_b])
            dve_tt(BM1, BM1, BM2, ALU.add, R_, [scan_b])
            for g8 in range(8):
                gg = 8 * q + g8
                S.op("dve", lambda e, g8=g8, gg=gg: e.tensor_tensor_scan(
                    out=BGS[:, g8 * 128:(g8 + 1) * 128], data0=SC[:, iRHO16, gg:gg + 1].to_broadcast([128, 128]),
                    data1=BM1[:, g8 * 128:(g8 + 1) * 128], initial=0.0, op0=ALU.mult, op1=ALU.add),
                    reads=R_ + [g_b], writes=[scan_b])
            S.op("act", lambda e: e.activation(out=BSW[0:64, :], in_=BGS[64:128, :], func=AF.Copy), reads=R_, writes=[scan_b])
            S.op("act", lambda e: e.activation(out=BSW[64:128, :], in_=BGS[0:64, :], func=AF.Copy), reads=R_, writes=[scan_b])
            dve_tt(BM1, BGS, EC, ALU.mult, R_, [scan_b])
            dve_tt(BM2, BSW, ES2, ALU.mult, R_, [scan_b])
            m3 = BM1.rearrange("p (g c) -> p g c", g=8)[:, :, 0:127]
            m4 = BM2.rearrange("p (g c) -> p g c", g=8)[:, :, 0:127]
            S.op("dve", lambda e, m3=m3, m4=m4: e.tensor_tensor(out=HP[:, :, 1:128], in0=m3, in1=m4, op=ALU.subtract),
                 reads=R_, writes=[HP_b])
            if q + 1 < 4:
                tables(q + 1)
            for gp in range(4):
                bank = 2 + (gp % 2)
                fns = []
                for k in range(2):
                    g8 = gp * 2 + k
                    osl = PS[bank][:, k * 256:(k + 1) * 256]
                    fns.append(lambda e, g8=g8, osl=osl: e.matmul(osl, lhsT=UL[:, g8, 0, :], rhs=Wintra[:, g8, 0, :], start=True, stop=False))
                    fns.append(lambda e, g8=g8, osl=osl: e.matmul(osl, lhsT=UL[:, g8, 1, :], rhs=Wintra[:, g8, 1, :], start=False, stop=False))
                    fns.append(lambda e, g8=g8, osl=osl: e.matmul(osl, lhsT=HP[:, g8, :], rhs=WdS[:, g8, :], start=False, stop=True))
                S.group("pe", fns, reads=[UL_b, wts_b, HP_b], writes=[PS_b[bank]])
                src = PS[bank].rearrange("p (k t h) -> p k t h", k=2, t=16)
                dst = Ytok.rearrange("p t (g h) -> p g t h", g=8)[:, gp * 2:gp * 2 + 2, :, :]
                S.op("act", lambda e, src=src, dst=dst: e.activation(out=dst, in_=src, func=AF.Gelu_apprx_tanh),
                     reads=[PS_b[bank]], writes=[Yt_b])
            for hb in range(2):
                bank = 6 + hb
                pst = PS[bank].bitcast(BF16)
                fns = []
                for k in range(8):
                    t = hb * 8 + k
                    fns.append(lambda e, k=k, t=t, pst=pst: e.transpose(pst[:, k * 128:(k + 1) * 128], Ytok[:, t, :], ident[:]))
                S.group("pe", fns, reads=[Yt_b, ident_b], writes=[PS_b[bank]])
                S.op("dve", lambda e, hb=hb, q=q, pst=pst: e.tensor_copy(out=YgT[:, q, hb * 1024:(hb + 1) * 1024], in_=pst),
                     reads=[PS_b[bank]], writes=[YgT_b])
            if q + 1 < 4:
                prod_pool(q + 1)

    if upto in "CD" and not opts.get("skip_c", False):
        stage_c()
        S.barrier()

    def cast_load(dst, src, b):
        S.dma("pool", lambda e: e.dma_start(out=dst, in_=src), b, writes=[b])

    def stage_d():
        YS = carve(8192, [128, 4, S_LEN], BF16)
        YS_b = S.buf("YS")
        Wglu = carve(12288, [128, 4, 1024], BF16)
        Wab = carve(14336, [128, 4, 1024], BF16)
        Wsb = carve(16384, [128, 4, 1024], BF16)
        MX = carve(18432, [128, 8, S_LEN], BF16)
        MX_b = S.buf("MX")
        GT2 = [[carve(32256 + 1024 * (2 * p_ + i), [128, S_LEN], BF16) for i in range(2)] for p_ in range(2)]
        GT2_b = [[S.buf("gt") for i in range(2)] for p_ in range(2)]
        WG = [carve(28672 + 1024 * i, [128, NKT, 2, 128], BF16) for i in range(2)]
        WG_b = [S.buf("wg") for i in range(2)]
        TM = [carve(30720 + 512 * i, [128, 512], F32) for i in range(3)] + [carve(26624 + 512 * i, [128, 512], F32) for i in range(4)]
        TM_b = [S.buf("tm") for i in range(7)]
        Wo = carve(36352, [128, 8, 1024], BF16)
        wbr_b = S.buf("wbr")
        Wo_b = S.buf("Wo")
        cast_load(Wglu, w_glu_d.rearrange("(k p) c -> p k c", p=128), wbr_b)
        cast_load(Wab, w_ab_d.rearrange("(k p) c -> p k c", p=128), wbr_b)
        cast_load(Wsb, w_sb_d.rearrange("(k p) c -> p k c", p=128), wbr_b)

        def load_wg(dt):
            sl = dt % 2
            for br in range(2):
                c0 = 5120 + br * 1024 + dt * 128
                S.dma("pool", lambda e, sl=sl, br=br, c0=c0: e.dma_start(out=WG[sl][:, :, br, :], in_=w_in_k[:, :, c0:c0 + 128]),
                      WG_b[sl], writes=[WG_b[sl]])
        load_wg(0)
        cast_load(Wo, w_out_d.rearrange("(k p) c -> p k c", p=128), Wo_b)

        for f in range(4):
            for tc in range(4):
                csl = slice(tc * 512, (tc + 1) * 512)
                for which, bank in ((0, 0 + 2 * (tc % 2)), (1, 1 + 2 * (tc % 2))):
                    fns = []
                    col = (f + 4 * which) * 128
                    for kt in range(4):
                        fns.append(lambda e, kt=kt, col=col, bank=bank, csl=csl: e.matmul(
                            PS[bank], lhsT=Wglu[:, kt, col:col + 128], rhs=YgT[:, kt, csl], start=(kt == 0), stop=(kt == 3)))
                    S.group("pe", fns, reads=[wbr_b, YgT_b], writes=[PS_b[bank]])
                ba, bb_ = 0 + 2 * (tc % 2), 1 + 2 * (tc % 2)
                ti = tc % 2
                S.op("act", lambda e, bb_=bb_, ti=ti: e.activation(out=TM[ti], in_=PS[bb_], func=AF.Sigmoid),
                     reads=[PS_b[bb_]], writes=[TM_b[ti]])
                S.op("dve", lambda e, ba=ba, ti=ti, f=f, csl=csl: e.tensor_tensor(out=YS[:, f, csl], in0=PS[ba], in1=TM[ti], op=ALU.mult),
                     reads=[PS_b[ba], TM_b[ti]], writes=[YS_b])

        for dt in range(8):
            sl = dt % 2
            GT, GT_b = GT2[dt % 2], GT2_b[dt % 2]
            if dt + 1 < 8:
                load_wg(dt + 1)
            for br in range(2):
                for tc in range(4):
                    bank = 4 + ((br * 4 + tc) % 2)
                    fns = []
                    for kt in range(NKT):
                        fns.append(lambda e, kt=kt, sl=sl, br=br, bank=bank, tc=tc: e.matmul(
                            PS[bank], lhsT=WG[sl][:, kt, br, :], rhs=xT[:, kt, tc * 512:(tc + 1) * 512],
                            start=(kt == 0), stop=(kt == NKT - 1)))
                    S.group("pe", fns, reads=[WG_b[sl], xT_b], writes=[PS_b[bank]])
                    dst = GT[br].rearrange("p (r m) -> p r m", r=16)[:, :, 32 * tc:32 * tc + 32]
                    src = PS[bank].rearrange("p (m r) -> p r m", r=16)
                    S.op("act", lambda e, dst=dst, src=src, br=br, dt=dt: e.activation(
                        out=dst, in_=src, func=AF.Sigmoid, bias=bgate[:, br * 8 + dt:br * 8 + dt + 1]),
                        reads=[PS_b[bank], ssmc_b], writes=[GT_b[br]])
            for tc in range(4):
                csl = slice(tc * 512, (tc + 1) * 512)
                ba, bb_ = 0 + 2 * (tc % 2), 1 + 2 * (tc % 2)
                fns = []
                for kt in range(4):
                    fns.append(lambda e, kt=kt, ba=ba, csl=csl, dt=dt: e.matmul(
                        PS[ba], lhsT=Wab[:, kt, dt * 128:(dt + 1) * 128], rhs=attnT[:, kt, csl], start=(kt == 0), stop=(kt == 3)))
                S.group("pe", fns, reads=[wbr_b, attnT_b], writes=[PS_b[ba]])
                fns = []
                for kt in range(4):
                    fns.append(lambda e, kt=kt, bb_=bb_, csl=csl, dt=dt: e.matmul(
                        PS[bb_], lhsT=Wsb[:, kt, dt * 128:(dt + 1) * 128], rhs=YS[:, kt, csl], start=(kt == 0), stop=(kt == 3)))
                S.group("pe", fns, reads=[wbr_b, YS_b], writes=[PS_b[bb_]])
                t0i, t1i = 3 + 2 * (tc % 2), 4 + 2 * (tc % 2)
                S.op("dve", lambda e, ba=ba, csl=csl, GT=GT, t0i=t0i: e.tensor_tensor(out=TM[t0i], in0=PS[ba], in1=GT[0][:, csl], op=ALU.mult),
                     reads=[PS_b[ba], GT_b[0]], writes=[TM_b[t0i]])
                S.op("dve", lambda e, bb_=bb_, csl=csl, GT=GT, t1i=t1i: e.tensor_tensor(out=TM[t1i], in0=PS[bb_], in1=GT[1][:, csl], op=ALU.mult),
                     reads=[PS_b[bb_], GT_b[1]], writes=[TM_b[t1i]])
                S.op("pool", lambda e, dt=dt, csl=csl, t0i=t0i, t1i=t1i: e.tensor_tensor(out=MX[:, dt, csl], in0=TM[t0i], in1=TM[t1i], op=ALU.add),
                     reads=[TM_b[t0i], TM_b[t1i]], writes=[MX_b])
        return MX, MX_b, Wo, Wo_b

    def stage_e(MX, MX_b, Wo, Wo_b):
        hscr = nc.dram_tensor("hscr", [S_LEN, D], F32, kind="Internal").ap()
        hs_b = [S.buf("hs%d" % j) for j in range(16)]
        LNT = carve(0, [128, 4, D], F32)
        lnt_b = S.buf("lnt")
        S.dma("sp", lambda e: e.dma_start(out=LNT, in_=lnt_d.rearrange("p (k d) -> p k d", k=4)), lnt_b, writes=[lnt_b])
        Wdn = carve(4096, [128, 22, D], BF16)
        Wdn_b = S.buf("Wdn")
        HT1 = [carve(15360 + 1024 * i, [128, D], F32) for i in range(3)] + [carve(34944, [128, D], F32)]
        HT1_b = [S.buf("ht1") for i in range(4)]
        HBs = [carve(32768 + 512 * i, [128, 512], F32).bitcast(BF16) for i in range(2)]
        HBs_b = [S.buf("hb") for i in range(2)]
        STs = [carve(33792 + 32 * i, [128, 32], F32) for i in range(4)]
        STs_b = [S.buf("stats") for i in range(4)]
        wf_offs = [26624 + 2048 * i for i in range(3)] + [18432 + 2048 * i for i in range(4)]
        NWS = len(wf_offs)
        WF = [carve(o, [128, NKT, 2, 256], BF16) for o in wf_offs]
        WF_b = [S.buf("wf") for i in range(NWS)]
        AT = carve(32768, [128, 22, 512], BF16)
        AT_b = S.buf("AT")
        PRE = [carve(38400 + 1024 * i, [128, D], F32) for i in range(2)]
        PRE_b = [S.buf("pre") for i in range(2)]
        XJ = [carve(40448 + 1024 * i, [128, D], F32) for i in range(2)] + [carve(33920, [128, D], F32)]
        XJ_b = [S.buf("xj") for i in range(3)]
        SG = carve(42496, [128, 512], F32)
        SG_b = S.buf("sg")
        hT, hT_b = xT, xT_b
        x_rows = x_d.rearrange("(m r) d -> r m d", r=16)
        o_rows = out_d.rearrange("(m r) d -> r m d", r=16)
        wfg = w_fg_d.rearrange("(k p) c -> p k c", p=128)
        wfu = w_fu_d.rearrange("(k p) c -> p k c", p=128)
        NLOAD = 44

        def load_wf(k):
            f0 = 2 * (k % 11)
            sl = k % NWS
            S.dma("pool", lambda e, sl=sl, f0=f0: e.dma_start(out=WF[sl][:, :, 0, :], in_=wfg[:, :, f0 * 128:(f0 + 2) * 128]),
                  WF_b[sl], writes=[WF_b[sl]])
            S.dma("pool", lambda e, sl=sl, f0=f0: e.dma_start(out=WF[sl][:, :, 1, :], in_=wfu[:, :, f0 * 128:(f0 + 2) * 128]),
                  WF_b[sl], writes=[WF_b[sl]])

        def layer_norm(src, dst, gi, src_b, dst_b, par, phases="abc"):
            ST = STs[par]
            st_b = STs_b[par]
            stats = ST[:, 0:12].rearrange("p (c s) -> p c s", c=2)
            if "a" in phases:
                for c in range(2):
                    S.op("dve", lambda e, c=c: e.bn_stats(out=stats[:, c, :], in_=src[:, c * 512:(c + 1) * 512]),
                         reads=[src_b], writes=[st_b])
                S.op("dve", lambda e: e.bn_aggr(out=ST[:, 12:14], in_=ST[:, 0:12]), reads=[st_b], writes=[st_b])
                S.op("dve", lambda e: e.tensor_scalar(out=ST[:, 14:15], in0=ST[:, 13:14], scalar1=LN_EPS, scalar2=None, op0=ALU.add),
                     reads=[st_b], writes=[st_b])
                S.op("act", lambda e: e.activation(out=ST[:, 14:15], in_=ST[:, 14:15], func=AF.Sqrt), reads=[st_b], writes=[st_b])
            if "b" in phases:
                S.op("dve", lambda e: e.reciprocal(out=ST[:, 15:16], in_=ST[:, 14:15]), reads=[st_b], writes=[st_b])
                S.op("dve", lambda e: e.scalar_tensor_tensor(out=ST[:, 16:17], in0=ST[:, 12:13], scalar=-1.0, in1=ST[:, 15:16],
                                                             op0=ALU.mult, op1=ALU.mult), reads=[st_b], writes=[st_b])
                S.op("act", lambda e: e.activation(out=dst, in_=src, func=AF.Identity, scale=ST[:, 15:16], bias=ST[:, 16:17]),
                     reads=[src_b, st_b], writes=[dst_b])
            if "c" in phases:
                S.op("pool" if gi == 0 else "dve", lambda e: e.tensor_tensor(out=dst, in0=dst, in1=LNT[:, gi, :], op=ALU.mult),
                     reads=[lnt_b], writes=[dst_b])
                S.op("pool", lambda e: e.tensor_tensor(out=dst, in0=dst, in1=LNT[:, gi + 1, :], op=ALU.add),
                     reads=[lnt_b], writes=[dst_b])

        nloaded = [0]
        for k in range(3):
            load_wf(k)
            nloaded[0] += 1
        S.dma("pool", lambda e: e.dma_start(out=Wdn, in_=w_fd_d.rearrange("(f p) c -> p f c", p=128)), Wdn_b, writes=[Wdn_b])

        def e1_f0(j):
            xs = j % 3
            if j == 0:
                for j2 in range(2):
                    S.dma("sp", lambda e, j2=j2: e.dma_start(out=XJ[j2 % 3], in_=x_rows[j2]), XJ_b[j2 % 3], writes=[XJ_b[j2 % 3]])
            if j + 2 < 16:
                S.dma("sp", lambda e, j=j: e.dma_start(out=XJ[(j + 2) % 3], in_=x_rows[j + 2]), XJ_b[(j + 2) % 3], writes=[XJ_b[(j + 2) % 3]])
            ht = HT1[j % 4]
            ht_b = HT1_b[j % 4]
            for nh in range(2):
                bank = 2 * (j % 3) + nh
                fns = []
                for dt in range(8):
                    fns.append(lambda e, dt=dt, j=j, nh=nh, bank=bank: e.matmul(
                        PS[bank], lhsT=MX[:, dt, j * 128:(j + 1) * 128], rhs=Wo[:, dt, nh * 512:(nh + 1) * 512],
                        start=(dt == 0), stop=(dt == 7)))
                S.group("pe", fns, reads=[MX_b, Wo_b], writes=[PS_b[bank]])
                S.op("dve", lambda e, nh=nh, bank=bank, ht=ht, xs=xs: e.scalar_tensor_tensor(
                    out=ht[:, nh * 512:(nh + 1) * 512], in0=XJ[xs][:, nh * 512:(nh + 1) * 512], scalar=ALPHA,
                    in1=PS[bank], op0=ALU.mult, op1=ALU.add), reads=[XJ_b[xs], PS_b[bank]], writes=[ht_b])
            layer_norm(ht, ht, 0, ht_b, ht_b, j % 4, phases="a")

        def e1_f1(j):
            layer_norm(HT1[j % 4], HT1[j % 4], 0, HT1_b[j % 4], HT1_b[j % 4], j % 4, phases="b")

        def e1_f2(j):
            ht = HT1[j % 4]
            ht_b = HT1_b[j % 4]
            layer_norm(ht, ht, 0, ht_b, ht_b, j % 4, phases="c")
            S.dma("sp", lambda e, j=j, ht=ht: e.dma_start(out=hscr[j * 128:(j + 1) * 128, :], in_=ht), hs_b[j],
                  reads=[ht_b], writes=[hs_b[j]])
            HB = HBs[j % 2]
            S.op("act", lambda e, ht=ht, HB=HB: e.activation(out=HB, in_=ht, func=AF.Copy), reads=[ht_b], writes=[HBs_b[j % 2]])

        def e1_back(j):
            HB = HBs[j % 2]
            bank = 6 + (j % 2)
            pst = PS[bank].bitcast(BF16)
            fns = []
            for kt in range(NKT):
                fns.append(lambda e, kt=kt, pst=pst, HB=HB: e.transpose(pst[:, kt * 128:(kt + 1) * 128], HB[:, kt * 128:(kt + 1) * 128], ident[:]))
            S.group("pe", fns, reads=[HBs_b[j % 2], ident_b], writes=[PS_b[bank]])
            S.op("act", lambda e, j=j, pst=pst: e.activation(out=hT[:, :, j * 128:(j + 1) * 128], in_=pst.rearrange("p (k t) -> p k t", k=NKT), func=AF.Copy),
                 reads=[PS_b[bank]], writes=[hT_b])

        for i in range(16 + 3):
            if i < 16:
                e1_f0(i)
            if 0 <= i - 1 < 16:
                e1_f1(i - 1)
            if 0 <= i - 2 < 16:
                e1_f2(i - 2)
            if 0 <= i - 3 < 16:
                e1_back(i - 3)
        S.barrier()

        PF = NWS - 1
        kuse = 0
        for qt in range(opts.get('n_qt', 4)):
            tsl = slice(qt * 512, (qt + 1) * 512)
            for f in range(22):
                if f % 2 == 0:
                    while nloaded[0] < min(NLOAD, kuse + PF):
                        load_wf(nloaded[0])
                        nloaded[0] += 1
                sl = kuse % NWS
                fo = (f % 2) * 128
                bg, bu = (0, 1) if f % 2 == 0 else (2, 3)
                for which, bank in ((0, bg), (1, bu)):
                    fns = []
                    for kt in range(NKT):
                        fns.append(lambda e, kt=kt, sl=sl, which=which, bank=bank, tsl=tsl, fo=fo: e.matmul(
                            PS[bank], lhsT=WF[sl][:, kt, which, fo:fo + 128], rhs=hT[:, kt, tsl], start=(kt == 0), stop=(kt == NKT - 1)))
                    S.group("pe", fns, reads=[WF_b[sl], hT_b], writes=[PS_b[bank]])
                S.op("act", lambda e, bg=bg: e.activation(out=SG, in_=PS[bg], func=AF.Silu), reads=[PS_b[bg]], writes=[SG_b])
                S.op("dve", lambda e, bu=bu, f=f: e.tensor_tensor(out=AT[:, f, :], in0=PS[bu], in1=SG, op=ALU.mult),
                     reads=[PS_b[bu], SG_b], writes=[AT_b])
                if f % 2 == 1:
                    kuse += 1
            def load_h(jj):
                j = qt * 4 + jj
                xs = jj % 2
                S.dma("sp", lambda e, j=j, xs=xs: e.dma_start(out=XJ[xs], in_=hscr[j * 128:(j + 1) * 128, :]), XJ_b[xs],
                      reads=[hs_b[j]], writes=[XJ_b[xs]])
            load_h(0)
            load_h(1)
            for jj in range(4):
                j = qt * 4 + jj
                xs = jj % 2
                pre = PRE[jj % 2]
                pre_b = PRE_b[jj % 2]
                for nh in range(2):
                    bank = 4 + nh
                    fns = []
                    for f in range(22):
                        fns.append(lambda e, f=f, jj=jj, nh=nh, bank=bank: e.matmul(
                            PS[bank], lhsT=AT[:, f, jj * 128:(jj + 1) * 128], rhs=Wdn[:, f, nh * 512:(nh + 1) * 512],
                            start=(f == 0), stop=(f == 21)))
                    S.group("pe", fns, reads=[AT_b, Wdn_b], writes=[PS_b[bank]])
                    S.op("dve", lambda e, nh=nh, bank=bank, xs=xs, pre=pre: e.scalar_tensor_tensor(
                        out=pre[:, nh * 512:(nh + 1) * 512], in0=XJ[xs][:, nh * 512:(nh + 1) * 512], scalar=ALPHA,
                        in1=PS[bank], op0=ALU.mult, op1=ALU.add), reads=[XJ_b[xs], PS_b[bank]], writes=[pre_b])
                if jj + 2 < 4:
                    load_h(jj + 2)
                layer_norm(pre, pre, 2, pre_b, pre_b, jj % 2)
                out_toks.append(S.dma("sp", lambda e, j=j, pre=pre: e.dma_start(out=o_rows[j], in_=pre), pre_b, reads=[pre_b]))

    if upto in "D":
        MX, MX_b, Wo, Wo_b = stage_d()
        if not opts.get("skip_e", False):
            S.barrier()
            stage_e(MX, MX_b, Wo, Wo_b)

    dump_b = S.buf("dump")
    if dbg:
        S.barrier()
        for name in dbg:
            off, shape, dt = opts["dbg_src"][name]
            if off == "xT":
                t = xT[:]
            else:
                t = carve(off, shape, dt)
            out_toks.append(S.dma("sp", lambda e, name=name, t=t: e.dma_start(out=dbg_d[name], in_=t), dump_b))
    S.wait_only("sp", out_toks)
    S.build()
    es.close()
    return nc


_CACHE = {}


def _shared_inputs(inputs):
    f32 = np.float32
    c = _consts()
    a_re = np.asarray(inputs["ssm_a_re"], f32)[0]
    a_im = np.asarray(inputs["ssm_a_im"], f32)[0]
    ldt = np.asarray(inputs["ssm_log_dt"], f32)[0]
    b_re = np.asarray(inputs["ssm_b_re"], f32)[0]
    b_im = np.asarray(inputs["ssm_b_im"], f32)[0]
    c_re = np.asarray(inputs["ssm_c_re"], f32)[0]
    c_im = np.asarray(inputs["ssm_c_im"], f32)[0]
    dsk = np.asarray(inputs["ssm_d"], f32)[0]
    ar2 = np.tile(a_re.T, (2, 1))
    ai2 = np.tile(a_im.T, (2, 1))
    ldt2 = np.tile(ldt[None, :], (128, 1))
    dcol = np.tile(dsk.reshape(32, 16).T, (8, 1))
    bre = b_re.transpose(1, 0, 2).reshape(64, 512)
    bim = b_im.transpose(1, 0, 2).reshape(64, 512)
    cre = c_re.transpose(2, 0, 1).reshape(64, 512)
    cim = c_im.transpose(2, 0, 1).reshape(64, 512)
    BA = np.concatenate([bre, bim], 0)
    BB = np.concatenate([bim, bre], 0)
    CA = np.concatenate([cre, cim], 0)
    CB = np.concatenate([cim, cre], 0)
    ssmp = np.ascontiguousarray(np.concatenate([ar2, ai2, ldt2, dcol, BA, BB, CA, CB], axis=1).astype(f32))
    bg = np.asarray(inputs["b_gate"], f32)[0]
    bgate = np.ascontiguousarray(bg.reshape(2, 8, 128).transpose(2, 0, 1).reshape(128, 16))
    lnt = np.concatenate([np.asarray(inputs[k], f32)[0] for k in ("ln1_g", "ln1_b", "ln2_g", "ln2_b")])[None, :]
    lnt = np.ascontiguousarray(np.tile(lnt, (128, 1)))
    sh = {"ident": c["ident"], "cosT": c["cosT"], "sinT": c["sinT"], "ssmc": c["ssmc"], "ssmp": ssmp,
          "bgate": bgate, "lnt": lnt}
    for k in ("w_in", "w_glu", "w_attn_br", "w_ssm_br", "w_out", "w_ff_gate", "w_ff_up", "w_ff_down"):
        sh[k] = np.ascontiguousarray(np.asarray(inputs[k], f32)[0])
    return sh


def kernel(**inputs):
    x = np.asarray(inputs["x"], np.float32)
    nb = x.shape[0]
    if "nc" not in _CACHE:
        _CACHE["nc"] = build_program()
    nc = _CACHE["nc"]
    sh = _shared_inputs(inputs)
    in_maps = []
    for b in range(nb):
        m = dict(sh)
        m["x"] = np.ascontiguousarray(x[b])
        in_maps.append(m)
    res = run_bass_kernel_spmd(nc, in_maps, core_ids=list(range(nb)))
    out = np.stack([np.asarray(r["out"], np.float32) for r in res.results], axis=0)
    return out
```
